# Optimizing a Trainium2 kernel written in Bass

```python
import math
import jax, jax.numpy as jnp
from jax import lax
import numpy as np

D_MODEL = 1024
BATCH = 8
SEQ = 2048
DEPTH = 1
DEC_BATCH = 128
DEC_SEQ = 8
PAST_LEN = 16384
PAGE_SIZE = 128

SSM_EXPAND = 2
SSM_D_INNER = SSM_EXPAND * D_MODEL
SSM_HEAD_DIM = 64
SSM_HEADS = SSM_D_INNER // SSM_HEAD_DIM
SSM_GROUPS = 4
SSM_STATE = 128
SSM_CONV = 4
SSM_CHUNK = 128
SSM_CONV_DIM = SSM_D_INNER + 2 * SSM_GROUPS * SSM_STATE
RWKV_HEAD_DIM = 64
RWKV_DIM = D_MODEL
RWKV_HEADS = RWKV_DIM // RWKV_HEAD_DIM
W_LORA = 64
A_LORA = 64
G_LORA = 128
RWKV_SHIFT_DIM = 3 * RWKV_DIM + W_LORA + A_LORA + G_LORA
D_FF = 2816
FFN_CONV = 3
IN_DIM = SSM_D_INNER + SSM_CONV_DIM + SSM_HEADS + RWKV_SHIFT_DIM + 2 * D_MODEL
NORM_EPS = 1e-5
GN_EPS = 64e-5

kernel_name = 'hybrid_ssd_rwkv7_convglu_step'


def _rmsnorm(x, g):
    xf = x.astype(jnp.float32)
    y = xf * lax.rsqrt(jnp.mean(xf * xf, axis=-1, keepdims=True) + NORM_EPS)
    return (y * g.astype(jnp.float32)).astype(x.dtype)


def _causal_dwconv(u, buf, w, b):
    L = u.shape[1]
    K = w.shape[0]
    full = jnp.concatenate([buf.astype(u.dtype), u], axis=1)
    out = full[:, 0:L] * w[0]
    for j in range(1, K):
        out = out + full[:, j:j + L] * w[j]
    return out + b, full[:, full.shape[1] - (K - 1):]


def _token_shift(u, buf, mu):
    prev = jnp.concatenate([buf[:, None].astype(u.dtype), u[:, :-1]], axis=1)
    return u + (prev - u) * mu, u[:, -1]


def _ssd(xs, dt, A, Bm, Cm, h0):
    b, L, H, P = xs.shape
    G, N = Bm.shape[2], Bm.shape[3]
    Hg = H // G
    l = math.gcd(L, SSM_CHUNK)
    c = L // l
    xdt = (xs * dt[..., None]).reshape(b, c, l, G, Hg, P)
    cs = jnp.cumsum((dt * A).reshape(b, c, l, G, Hg), axis=2)
    Bc = Bm.reshape(b, c, l, G, N)
    Cc = Cm.reshape(b, c, l, G, N)
    causal = jnp.tril(jnp.ones((l, l), dtype=bool))
    seg = cs[:, :, :, None] - cs[:, :, None, :]
    decay_ls = jnp.exp(jnp.where(causal[None, None, :, :, None, None], seg, -jnp.inf))
    cb = jnp.einsum('bclgn,bcsgn->bclsg', Cc, Bc)
    y_diag = jnp.einsum('bclsgh,bcsghp->bclghp', cb[..., None] * decay_ls, xdt)
    to_end = jnp.exp(cs[:, :, -1:] - cs)
    chunk_states = jnp.einsum('bclgn,bclghp->bcghpn', Bc, xdt * to_end[..., None])
    chunk_decay = jnp.exp(cs[:, :, -1])

    def step(h, inp):
        st, dec = inp
        return h * dec[..., None, None] + st, h

    h_last, h_prev = lax.scan(step, h0.reshape(b, G, Hg, P, N),
                              (jnp.moveaxis(chunk_states, 1, 0), jnp.moveaxis(chunk_decay, 1, 0)))
    h_prev = jnp.moveaxis(h_prev, 0, 1)
    y_off = jnp.einsum('bclgn,bcghpn->bclghp', Cc, h_prev) * jnp.exp(cs)[..., None]
    y = (y_diag + y_off).reshape(b, L, H, P)
    return y, h_last.reshape(b, H, P, N)


def _rwkv7_scan(r, decay, k, v, kk, a, S0):
    def step(S, inp):
        r_t, w_t, k_t, v_t, kk_t, a_t = inp
        sa = jnp.einsum('bhvk,bhk->bhv', S, -kk_t)
        S = (S * w_t[:, :, None, :] + sa[..., None] * (kk_t * a_t)[:, :, None, :]
             + v_t[..., None] * k_t[:, :, None, :])
        return S, jnp.einsum('bhvk,bhk->bhv', S, r_t)

    xs = tuple(jnp.moveaxis(t, 1, 0) for t in (r, decay, k, v, kk, a))
    S, ys = lax.scan(step, S0, xs)
    return jnp.moveaxis(ys, 0, 1), S


def _layer(x, conv_buf, ssm_state, shift_buf, wkv_state, ffn_buf, lp):
    f32 = jnp.float32
    b, L, _ = x.shape
    G, N = SSM_GROUPS, SSM_STATE
    H, K = RWKV_HEADS, RWKV_HEAD_DIM
    h = _rmsnorm(x, lp['norm1_g'])
    proj = h @ lp['w_in']
    o1 = SSM_D_INNER
    o2 = o1 + SSM_CONV_DIM
    o3 = o2 + SSM_HEADS
    o4 = o3 + RWKV_SHIFT_DIM
    z, xbc, dt_raw, rw, gates_raw = jnp.split(proj, [o1, o2, o3, o4], axis=-1)

    xbc_c, new_conv = _causal_dwconv(xbc, conv_buf, lp['ssm_conv_w'], lp['ssm_conv_b'])
    xbc_c = jax.nn.silu(xbc_c.astype(f32))
    xm, Bm, Cm = jnp.split(xbc_c, [SSM_D_INNER, SSM_D_INNER + G * N], axis=-1)
    dt = jax.nn.softplus(dt_raw.astype(f32) + lp['ssm_dt_bias'].astype(f32))
    A = -jnp.exp(lp['ssm_a_log'].astype(f32))
    xm4 = xm.reshape(b, L, SSM_HEADS, SSM_HEAD_DIM)
    ya, new_ssm = _ssd(xm4, dt, A, Bm.reshape(b, L, G, N), Cm.reshape(b, L, G, N),
                       ssm_state.astype(f32))
    ya = ya + lp['ssm_d'].astype(f32)[:, None] * xm4
    ya = ya.reshape(b, L, G, SSM_D_INNER // G) * jax.nn.silu(z.astype(f32)).reshape(b, L, G, SSM_D_INNER // G)
    ya = ya * lax.rsqrt(jnp.mean(ya * ya, axis=-1, keepdims=True) + NORM_EPS)
    ya = ya.reshape(b, L, SSM_D_INNER) * lp['ssm_norm_g'].astype(f32)
    u_a = ya.astype(x.dtype) @ lp['w_branch_a']

    rw_mix, new_shift = _token_shift(rw, shift_buf, lp['rwkv_mu'])
    r, k, v, wl, al, gl = jnp.split(
        rw_mix.astype(f32),
        [RWKV_DIM, 2 * RWKV_DIM, 3 * RWKV_DIM, 3 * RWKV_DIM + W_LORA, 3 * RWKV_DIM + W_LORA + A_LORA],
        axis=-1)
    wlog = -jax.nn.softplus(-(lp['rwkv_w0'] + jnp.tanh(wl) @ lp['rwkv_w_up'])) - 0.5
    decay = jnp.exp(-jnp.exp(wlog.astype(f32)))
    a = jax.nn.sigmoid(lp['rwkv_a0'] + al @ lp['rwkv_a_up']).astype(f32)
    g = (jax.nn.sigmoid(gl) @ lp['rwkv_g_up']).astype(f32)
    hd = lambda t: t.reshape(b, L, H, K)
    kk = hd(k * lp['rwkv_k_k']).astype(f32)
    kk = kk / jnp.maximum(jnp.sqrt(jnp.sum(kk * kk, axis=-1, keepdims=True)), 1e-12)
    k = (k * (1.0 + (a - 1.0) * lp['rwkv_k_a'])).astype(f32)
    r4, k4, v4, a4 = hd(r), hd(k), hd(v), hd(a)
    yb, new_wkv = _rwkv7_scan(r4, hd(decay), k4, v4, kk, a4, wkv_state.astype(f32))
    mu = jnp.mean(yb, axis=-1, keepdims=True)
    var = jnp.mean((yb - mu) ** 2, axis=-1, keepdims=True)
    yb = ((yb - mu) * lax.rsqrt(var + GN_EPS) * lp['rwkv_ln_w'].astype(f32).reshape(H, K)
          + lp['rwkv_ln_b'].astype(f32).reshape(H, K))
    yb = yb + jnp.sum(r4 * k4 * lp['rwkv_r_k'].astype(f32), axis=-1, keepdims=True) * v4
    yb = yb.reshape(b, L, RWKV_DIM) * g
    u_b = yb.astype(x.dtype) @ lp['w_branch_b']

    gates = jax.nn.sigmoid(gates_raw.astype(f32))
    ga, gb = jnp.split(gates, 2, axis=-1)
    m = (ga * u_a.astype(f32) + gb * u_b.astype(f32)).astype(x.dtype)
    x = x + m @ lp['w_out']

    h2 = _rmsnorm(x, lp['norm2_g'])
    up = h2 @ lp['ffn_w_up']
    ug, uv = jnp.split(up, 2, axis=-1)
    ug, new_ffn = _causal_dwconv(ug, ffn_buf, lp['ffn_conv_w'], lp['ffn_conv_b'])
    x = x + (jax.nn.silu(ug) * uv) @ lp['ffn_w_down']
    return x, (new_conv, new_ssm, new_shift, new_wkv, new_ffn)


def setup_inputs(seed: int = 0) -> dict:
    key = jax.random.key(seed)
    ks = iter(jax.random.split(key, 48))
    f32 = jnp.float32

    def nrm(shape, scale):
        return scale * jax.random.normal(next(ks), shape, f32)

    def unif(shape, lo, hi):
        return jax.random.uniform(next(ks), shape, f32, minval=lo, maxval=hi)

    Dp = DEPTH
    dt0 = jnp.exp(unif((Dp, SSM_HEADS), math.log(1e-3), math.log(1e-1)))
    return {
        'x_prompt': nrm((BATCH, SEQ, D_MODEL), 1.0),
        'x_sample': nrm((DEC_BATCH, DEC_SEQ, D_MODEL), 1.0),
        'state_ssm_conv': nrm((Dp, DEC_BATCH, SSM_CONV - 1, SSM_CONV_DIM), 1.0),
        'state_ssm': nrm((Dp, DEC_BATCH, SSM_HEADS, SSM_HEAD_DIM, SSM_STATE), 0.1),
        'state_rwkv_shift': nrm((Dp, DEC_BATCH, RWKV_SHIFT_DIM), 1.0),
        'state_rwkv': nrm((Dp, DEC_BATCH, RWKV_HEADS, RWKV_HEAD_DIM, RWKV_HEAD_DIM), 0.1),
        'state_ffn_conv': nrm((Dp, DEC_BATCH, FFN_CONV - 1, D_FF), 1.0),
        'norm1_g': 1.0 + nrm((Dp, D_MODEL), 0.02),
        'w_in': nrm((Dp, D_MODEL, IN_DIM), D_MODEL ** -0.5),
        'ssm_conv_w': nrm((Dp, SSM_CONV, SSM_CONV_DIM), SSM_CONV ** -0.5),
        'ssm_conv_b': nrm((Dp, SSM_CONV_DIM), 0.02),
        'ssm_dt_bias': dt0 + jnp.log(-jnp.expm1(-dt0)),
        'ssm_a_log': jnp.log(unif((Dp, SSM_HEADS), 1.0, 16.0)),
        'ssm_d': 1.0 + nrm((Dp, SSM_HEADS), 0.1),
        'ssm_norm_g': 1.0 + nrm((Dp, SSM_D_INNER), 0.02),
        'w_branch_a': nrm((Dp, SSM_D_INNER, D_MODEL), SSM_D_INNER ** -0.5),
        'rwkv_mu': unif((Dp, RWKV_SHIFT_DIM), 0.0, 1.0),
        'rwkv_w0': -0.6 + nrm((Dp, RWKV_DIM), 0.3),
        'rwkv_w_up': nrm((Dp, W_LORA, RWKV_DIM), 0.1 * W_LORA ** -0.5),
        'rwkv_a0': nrm((Dp, RWKV_DIM), 0.1),
        'rwkv_a_up': nrm((Dp, A_LORA, RWKV_DIM), 0.1 * A_LORA ** -0.5),
        'rwkv_g_up': nrm((Dp, G_LORA, RWKV_DIM), G_LORA ** -0.5),
        'rwkv_k_k': 0.85 + nrm((Dp, RWKV_DIM), 0.02),
        'rwkv_k_a': 1.0 + nrm((Dp, RWKV_DIM), 0.02),
        'rwkv_r_k': nrm((Dp, RWKV_HEADS, RWKV_HEAD_DIM), 0.1),
        'rwkv_ln_w': 1.0 + nrm((Dp, RWKV_DIM), 0.02),
        'rwkv_ln_b': nrm((Dp, RWKV_DIM), 0.02),
        'w_branch_b': nrm((Dp, RWKV_DIM, D_MODEL), RWKV_DIM ** -0.5),
        'w_out': nrm((Dp, D_MODEL, D_MODEL), D_MODEL ** -0.5),
        'norm2_g': 1.0 + nrm((Dp, D_MODEL), 0.02),
        'ffn_w_up': nrm((Dp, D_MODEL, 2 * D_FF), D_MODEL ** -0.5),
        'ffn_conv_w': nrm((Dp, FFN_CONV, D_FF), FFN_CONV ** -0.5),
        'ffn_conv_b': nrm((Dp, D_FF), 0.02),
        'ffn_w_down': nrm((Dp, D_FF, D_MODEL), D_FF ** -0.5),
        'final_g': 1.0 + nrm((D_MODEL,), 0.02),
    }


def reference(x_prompt, x_sample, state_ssm_conv, state_ssm, state_rwkv_shift, state_rwkv,
              state_ffn_conv, norm1_g, w_in, ssm_conv_w, ssm_conv_b, ssm_dt_bias, ssm_a_log,
              ssm_d, ssm_norm_g, w_branch_a, rwkv_mu, rwkv_w0, rwkv_w_up, rwkv_a0, rwkv_a_up,
              rwkv_g_up, rwkv_k_k, rwkv_k_a, rwkv_r_k, rwkv_ln_w, rwkv_ln_b, w_branch_b, w_out,
              norm2_g, ffn_w_up, ffn_conv_w, ffn_conv_b, ffn_w_down, final_g):
    xp, xs = x_prompt, x_sample
    bp = xp.shape[0]
    new_p = ([], [], [], [], [])
    new_s = ([], [], [], [], [])
    for i in range(DEPTH):
        lp = {
            'norm1_g': norm1_g[i], 'w_in': w_in[i], 'ssm_conv_w': ssm_conv_w[i],
            'ssm_conv_b': ssm_conv_b[i], 'ssm_dt_bias': ssm_dt_bias[i], 'ssm_a_log': ssm_a_log[i],
            'ssm_d': ssm_d[i], 'ssm_norm_g': ssm_norm_g[i], 'w_branch_a': w_branch_a[i],
            'rwkv_mu': rwkv_mu[i], 'rwkv_w0': rwkv_w0[i], 'rwkv_w_up': rwkv_w_up[i],
            'rwkv_a0': rwkv_a0[i], 'rwkv_a_up': rwkv_a_up[i], 'rwkv_g_up': rwkv_g_up[i],
            'rwkv_k_k': rwkv_k_k[i], 'rwkv_k_a': rwkv_k_a[i], 'rwkv_r_k': rwkv_r_k[i],
            'rwkv_ln_w': rwkv_ln_w[i], 'rwkv_ln_b': rwkv_ln_b[i], 'w_branch_b': w_branch_b[i],
            'w_out': w_out[i], 'norm2_g': norm2_g[i], 'ffn_w_up': ffn_w_up[i],
            'ffn_conv_w': ffn_conv_w[i], 'ffn_conv_b': ffn_conv_b[i], 'ffn_w_down': ffn_w_down[i],
        }
        xp, sp = _layer(
            xp,
            jnp.zeros((bp, SSM_CONV - 1, SSM_CONV_DIM), xp.dtype),
            jnp.zeros((bp, SSM_HEADS, SSM_HEAD_DIM, SSM_STATE), jnp.float32),
            jnp.zeros((bp, RWKV_SHIFT_DIM), xp.dtype),
            jnp.zeros((bp, RWKV_HEADS, RWKV_HEAD_DIM, RWKV_HEAD_DIM), jnp.float32),
            jnp.zeros((bp, FFN_CONV - 1, D_FF), xp.dtype),
            lp)
        xs, ss = _layer(xs, state_ssm_conv[i], state_ssm[i], state_rwkv_shift[i], state_rwkv[i],
                        state_ffn_conv[i], lp)
        for j in range(5):
            new_p[j].append(sp[j])
            new_s[j].append(ss[j])
    y_prompt = _rmsnorm(xp, final_g)
    y_sample = _rmsnorm(xs, final_g)
    return (y_prompt, y_sample,
            jnp.stack(new_p[0]), jnp.stack(new_p[1]), jnp.stack(new_p[2]), jnp.stack(new_p[3]), jnp.stack(new_p[4]),
            jnp.stack(new_s[0]), jnp.stack(new_s[1]), jnp.stack(new_s[2]), jnp.stack(new_s[3]), jnp.stack(new_s[4]))
```

```python
import contextlib
import numpy as np
import concourse.bass as bass
import concourse.mybir as mybir
from concourse.bass_utils import run_bass_kernel_spmd

F32 = mybir.dt.float32
BF16 = mybir.dt.bfloat16
AF = mybir.ActivationFunctionType
ALU = mybir.AluOpType

NCORES = 8
OFF_Z, OFF_XBC, OFF_DT, OFF_RW, OFF_G = 0, 2048, 5120, 5152, 8480
IN_DIM = 10528
D_FF = 2816
C_ID, C_SUP, C_TRIP, C_UUP, C_SUS, C_TRIS, C_UUS, C_ONES, C_BLK, C_RST, C_SEQ = [128 * i for i in range(11)]
NCST = C_SEQ + 16
PR = {}
_o = 0
for _n, _r in (("n1g", 8), ("n2g", 8), ("cw", 96), ("cb", 24), ("mu", 26), ("fw", 66), ("fb", 22),
               ("w0", 8), ("a0", 8), ("kk", 8), ("ka", 8), ("rk", 8), ("lnw", 8), ("lnb", 8), ("ng", 16)):
    PR[_n] = _o
    _o += _r
NPR = _o
PB_DTB, PB_ALOG, PB_D, PB_FG = 0, 32, 64, 96
NPB = 96 + 1024


class Buf:
    def __init__(self, name, t):
        self.name = name
        self.t = t
        self.lw = None
        self.rd = []
        self.dkey = None
        self.dcnt = 0

    def __getitem__(self, k):
        return self.t[k]


class Prog:
    ENGS = ["pe", "act", "dve", "pool", "sp"]
    STRICT_ENGS = ("act", "dve", "pool")

    def __init__(self, nc):
        self.nc = nc
        self.es = contextlib.ExitStack()
        self.ops = {e: [] for e in self.ENGS}
        self.cnt = {e: 0 for e in self.ENGS}
        self.sems = {}
        self.reg = {}
        for e in self.ENGS:
            self.sems[e] = self.es.enter_context(nc.semaphore("s_" + e))
        self.banks = []
        self.bi = 0

    def sb(self, name, shape, dt=F32):
        t = self.es.enter_context(self.nc.sbuf_tensor(name, list(shape), dt))
        b = Buf(name, t)
        self.reg[name] = b
        return b

    def ps(self, name, shape, dt=F32):
        t = self.es.enter_context(self.nc.psum_tensor(name, list(shape), dt))
        b = Buf(name, t)
        self.reg[name] = b
        return b

    def bank(self):
        b = self.banks[self.bi % len(self.banks)]
        self.bi += 1
        return b

    def _dsem(self, buf):
        if buf.dkey is None:
            buf.dkey = "d_" + buf.name
            self.sems[buf.dkey] = self.es.enter_context(self.nc.semaphore(buf.dkey))
        return buf.dkey

    def _bufs(self, aps):
        out = []
        for a in aps:
            if a is None or isinstance(a, (int, float)):
                continue
            b = self.reg.get(a.name)
            if b is not None and b not in out:
                out.append(b)
        return out

    def _deps(self, eng, reads, writes, is_dma):
        w = {}

        def add(tok):
            if tok is None:
                return
            k, v = tok
            if k == eng and not is_dma and eng not in self.STRICT_ENGS:
                return
            if w.get(k, 0) < v:
                w[k] = v

        for b in reads:
            add(b.lw)
        for b in writes:
            add(b.lw)
            for t in b.rd:
                add(t)
        return w

    def op(self, eng, fn, rd, wr):
        reads = self._bufs(rd)
        writes = self._bufs(wr)
        w = self._deps(eng, reads, writes, False)
        self.cnt[eng] += 1
        tok = (eng, self.cnt[eng])
        for b in reads:
            if b not in writes:
                b.rd.append(tok)
        for b in writes:
            b.lw = tok
            b.rd = []
        self.ops[eng].append((w, fn, (eng, 1)))

    def dma(self, q, out_ap, in_ap, **kw):
        reads = self._bufs([in_ap])
        writes = self._bufs([out_ap])
        sembuf = (writes + reads)[0]
        w = self._deps(q, reads, writes, True)
        key = self._dsem(sembuf)
        sembuf.dcnt += 1
        tok = (key, 16 * sembuf.dcnt)
        for b in reads:
            b.rd.append(tok)
        for b in writes:
            b.lw = tok
            b.rd = []
        self.ops[q].append((w, lambda e: e.dma_start(out=out_ap, in_=in_ap, **kw), (key, 16)))
        return sembuf

    def final_wait(self, eng, bufs):
        w = {}
        for b in bufs:
            for tok in ([b.lw] + b.rd):
                if tok is None:
                    continue
                k, v = tok
                if w.get(k, 0) < v:
                    w[k] = v
        self.ops[eng].append((w, None, None))

    def mm(self, out, lhsT, rhs, start=True, stop=True):
        self.op("pe", lambda e: e.matmul(out, lhsT=lhsT, rhs=rhs, start=start, stop=stop), [lhsT, rhs], [out])

    def tr(self, out, in_, ident):
        self.op("pe", lambda e: e.transpose(out=out, in_=in_, identity=ident), [in_, ident], [out])

    def act(self, out, in_, func, bias=None, scale=None, accum=None):
        kw = {}
        if bias is not None:
            kw["bias"] = bias
        if scale is not None:
            kw["scale"] = scale
        if accum is not None:
            kw["accum_out"] = accum
        self.op("act", lambda e: e.activation(out=out, in_=in_, func=func, **kw), [in_, bias, scale], [out, accum])

    def tt(self, out, a, b, op, eng="dve"):
        self.op(eng, lambda e: e.tensor_tensor(out=out, in0=a, in1=b, op=op), [a, b], [out])

    def ts(self, out, a, s1, op0, s2=None, op1=None, eng="dve"):
        if op1 is None:
            self.op(eng, lambda e: e.tensor_scalar(out=out, in0=a, scalar1=s1, scalar2=None, op0=op0), [a, s1], [out])
        else:
            self.op(eng, lambda e: e.tensor_scalar(out=out, in0=a, scalar1=s1, scalar2=s2, op0=op0, op1=op1),
                    [a, s1, s2], [out])

    def stt(self, out, a, scalar, b, op0, op1, eng="dve"):
        self.op(eng, lambda e: e.scalar_tensor_tensor(out=out, in0=a, scalar=scalar, in1=b, op0=op0, op1=op1),
                [a, scalar, b], [out])

    def cp(self, out, in_, eng="dve"):
        if eng == "act":
            self.act(out, in_, AF.Copy)
        else:
            self.op(eng, lambda e: e.tensor_copy(out=out, in_=in_), [in_], [out])

    def ms(self, out, val, eng="dve"):
        self.op(eng, lambda e: e.memset(out, val), [], [out])

    def scan(self, out, d0, d1):
        self.op("dve", lambda e: e.tensor_tensor_scan(out=out, data0=d0, data1=d1, initial=0.0,
                                                      op0=ALU.mult, op1=ALU.add), [d0, d1], [out])

    def emit(self):
        nc = self.nc
        prog = self
        waited_seq = {e: set() for e in self.ENGS}
        for e in self.ENGS:
            for (w, fn, inc) in self.ops[e]:
                for k, v in w.items():
                    if k in waited_seq:
                        waited_seq[k].add(v)
        rank = {}
        for e in self.ENGS:
            rank[e] = {v: i + 1 for i, v in enumerate(sorted(waited_seq[e]))}

        def run(engname, e):
            waited = {}
            seq = 0
            for (w, fn, inc) in prog.ops[engname]:
                for k, v in w.items():
                    vv = rank[k][v] if k in rank else v
                    if waited.get(k, 0) < vv:
                        e.wait_ge(prog.sems[k], vv)
                        waited[k] = vv
                if fn is not None:
                    ins = fn(e)
                    if inc[0] in rank:
                        seq += 1
                        if seq in rank[inc[0]]:
                            ins.then_inc(prog.sems[inc[0]], 1)
                    else:
                        ins.then_inc(prog.sems[inc[0]], inc[1])

        with nc.Block() as block:
            @block.tensor
            def _(e):
                run("pe", e)

            @block.scalar
            def _(e):
                run("act", e)

            @block.vector
            def _(e):
                run("dve", e)

            @block.gpsimd
            def _(e):
                run("pool", e)

            @block.sync
            def _(e):
                run("sp", e)
        self.es.close()


def bc(ap, shape, axis):
    return ap.unsqueeze(axis).to_broadcast(list(shape))


def build_nc(NPT=16, do_sample=True):
    nc = bass.Bass("TRN2", target_bir_lowering=False)

    def din(name, shape):
        return nc.dram_tensor(name, list(shape), F32, kind="ExternalInput").ap()

    def dout(name, shape):
        return nc.dram_tensor(name, list(shape), F32, kind="ExternalOutput").ap()

    xp = din("xp", [2048, 1024])
    xs = din("xs", [128, 1024])
    s_conv = din("s_conv", [16, 3, 3072])
    s_ssm = din("s_ssm", [16, 32, 64, 128])
    s_shift = din("s_shift", [16, 3328])
    s_rwkv = din("s_rwkv", [16, 16, 64, 64])
    s_ffn = din("s_ffn", [16, 2, 2816])
    W = {"in": din("w_in", [1024, IN_DIM]), "a": din("w_a", [2048, 1024]), "b": din("w_b", [1024, 1024]),
         "out": din("w_out", [1024, 1024]), "up": din("w_up", [1024, 2 * D_FF]), "down": din("w_down", [D_FF, 1024])}
    lora_wa = din("lora_wa", [128, 1024])
    lora_g = din("lora_g", [128, 1024])
    prow = din("prow", [NPR, 128])
    pbc_d = din("pbc_in", [NPB])
    cst_d = din("cst_in", [128, NCST])

    y_p = dout("y_p", [2048, 1024])
    y_s = dout("y_s", [128, 1024])
    o_conv_p = dout("o_conv_p", [3, 3072])
    o_ssm_p = dout("o_ssm_p", [32, 64, 128])
    o_shift_p = dout("o_shift_p", [1, 3328])
    o_rwkv_p = dout("o_rwkv_p", [16, 64, 64])
    o_ffn_p = dout("o_ffn_p", [2, 2816])
    o_conv_s = dout("o_conv_s", [16, 3, 3072])
    o_ssm_s = dout("o_ssm_s", [16, 32, 64, 128])
    o_shift_s = dout("o_shift_s", [16, 3328])
    o_rwkv_s = dout("o_rwkv_s", [16, 16, 64, 64])
    o_ffn_s = dout("o_ffn_s", [16, 2, 2816])

    p = Prog(nc)
    for i in range(8):
        p.banks.append(p.ps(f"pb{i}", [128, 512]))

    cst = p.sb("cst", [128, NCST])
    identb = p.sb("identb", [128, 128], BF16)
    UUb = p.sb("UUb", [128, 2, 128], BF16)
    BLKb = p.sb("BLKb", [128, 128], BF16)
    parF = p.sb("parF", [128, NPR])
    pbc = p.sb("pbc", [128, NPB])
    Ab = p.sb("Ab", [128, 32])
    wa_up = p.sb("wa_up", [128, 1024], BF16)
    g_up = p.sb("g_up", [128, 1024], BF16)
    NS = 5
    slots = [p.sb(f"ws{i}", [128, 8, 512], BF16) for i in range(NS)]
    xres = [p.sb(f"xres{i}", [128, 1024]) for i in range(2)]
    hT = p.sb("hT", [128, 8, 128], BF16)
    projT = p.sb("projT", [128, 26, 176])
    carry_conv = p.sb("carry_conv", [128, 24, 3])
    carry_shift = p.sb("carry_shift", [128, 26, 1])
    carry_ffn = p.sb("carry_ffn", [128, 22, 2])
    xcT = p.sb("xcT", [128, 24, 128], BF16)
    zs2 = p.sb("zs2", [128, 2048], BF16)
    uaT = p.sb("uaT", [128, 8, 128], BF16)
    ubT = p.sb("ubT", [128, 8, 128], BF16)
    ybT = p.sb("ybT", [128, 8, 128], BF16)
    ugblk = [p.sb(f"ugblk{i}", [128, 4, 160]) for i in range(2)]
    sm_r = p.sb("sm_r", [128, 4])
    sm_d = p.sb("sm_d", [128, 160])
    sm_q = p.sb("sm_q", [128, 8])
    sm_g = p.sb("sm_g", [128, 128])
    sm_p = p.sb("sm_p", [128, 256])
    BG = [p.sb(f"BG{i}", [128, 2048]) for i in range(4)]
    HB = [p.sb(f"HB{i}", [128, 2048], BF16) for i in range(4)]
    yaT = xcT[:, 0:16, :]
    mT = ybT
    actT = BG[1][:].bitcast(BF16)[:, 0:2816].rearrange("p (c t) -> p c t", t=128)
    hb = HB[2][:, 0:1024]
    rowsbuf = HB[3][:].bitcast(F32)
    G16 = p.sb("G16", [128, 8, 128], BF16)
    lT = p.sb("lT", [128, 2, 128])
    lb = p.sb("lb", [128, 3, 128], BF16)
    B_tok = p.sb("B_tok", [128, 512], BF16)
    MTs = [p.sb(f"MT{i}", [128, 8, 128], BF16) for i in range(2)]
    cbms = [p.sb(f"cbm{i}", [128, 128]) for i in range(2)]
    MT = MTs[0]
    hst = p.sb("hst", [128, 2048])
    hTbf = p.sb("hTbf", [128, 2048], BF16)
    Hst = p.sb("Hst", [128, 8, 64])
    Hbf = p.sb("Hbf", [128, 8, 64], BF16)
    NYb = [p.sb(f"NY{i}", [128, 4, 2, 128], BF16) for i in range(2)]
    AYb = [p.sb(f"AY{i}", [128, 4, 2, 128], BF16) for i in range(2)]
    Nq = [[p.sb(f"Nq{i}{k}", [128, 4, 128], BF16) for k in range(2)] for i in range(2)]
    Mq = [[p.sb(f"Mq{i}{k}", [128, 4, 128], BF16) for k in range(2)] for i in range(2)]
    Rb = [p.sb(f"Rb{i}", [128, 4, 128], BF16) for i in range(2)]
    W_sbs = [p.sb(f"W_sb{i}", [128, 4, 64], BF16) for i in range(2)]
    U_sbs = [p.sb(f"U_sb{i}", [128, 4, 64], BF16) for i in range(2)]
    W1T = p.sb("W1T", [128, 2, 128], BF16)

    ID = cst[:, C_ID:C_ID + 128]
    ONES = cst[:, C_ONES:C_ONES + 128]
    BLK = cst[:, C_BLK:C_BLK + 128]
    SEQ = cst[:, C_SEQ:C_SEQ + 16]

    def par(name, i=0, n=1):
        return parF[:, PR[name] + i:PR[name] + i + n]

    p.dma("sp", cst[:], cst_d)
    p.dma("sp", pbc[:], pbc_d.partition_broadcast(128))
    p.dma("pool", wa_up[:], lora_wa)
    p.dma("pool", g_up[:], lora_g)
    p.cp(identb[:], ID)
    p.cp(BLKb[:], cst[:, C_BLK:C_BLK + 128])
    p.cp(UUb[:, 0, :], cst[:, C_UUP:C_UUP + 128])
    p.cp(UUb[:, 1, :], cst[:, C_UUS:C_UUS + 128])
    p.act(Ab[:], pbc[:, PB_ALOG:PB_ALOG + 32], AF.Exp)
    p.ts(Ab[:], Ab[:], -1.0, ALU.mult)
    r0 = 0
    while r0 < NPR:
        nr = min(128, NPR - r0)
        st = BG[0]
        p.dma("sp", st[0:nr, 0:128], prow[r0:r0 + nr, :])
        pb = p.bank()
        p.mm(pb[:, 0:nr], lhsT=st[0:nr, 0:128], rhs=cst[0:nr, C_ID:C_ID + nr])
        p.cp(parF[:, r0:r0 + nr], pb[:, 0:nr])
        r0 += nr
    for nm, n in (("cw", 96), ("cb", 24), ("fw", 66), ("fb", 22), ("w0", 8), ("a0", 8)):
        p.ts(par(nm, 0, n), par(nm, 0, n), 0.5, ALU.mult)
    p.ms(hst[:], 0.0)
    p.ms(hTbf[:], 0.0)
    p.ms(Hst[:], 0.0)
    p.ms(Hbf[:], 0.0)
    p.ms(carry_conv[:], 0.0)
    p.ms(carry_shift[:], 0.0)
    p.ms(carry_ffn[:], 0.0)

    def tile_plan():
        pl = []
        for i in range(6):
            pl.append(("in", 0, 8, OFF_XBC + 512 * i, 512))
        pl.append(("in", 0, 8, OFF_DT, 32))
        for i in range(4):
            pl.append(("in", 0, 8, OFF_Z + 512 * i, 512))
        for i in range(6):
            pl.append(("in", 0, 8, OFF_RW + 512 * i, 512))
        pl.append(("in", 0, 8, OFF_RW + 3072, 256))
        for cb_ in range(2):
            for kb in range(2):
                pl.append(("a", 8 * kb, 8, 512 * cb_, 512))
        for cb_ in range(2):
            pl.append(("b", 0, 8, 512 * cb_, 512))
        for i in range(4):
            pl.append(("in", 0, 8, OFF_G + 512 * i, 512))
        for cb_ in range(2):
            pl.append(("out", 0, 8, 512 * cb_, 512))
        for i in range(6):
            n = 512 if i < 5 else 256
            pl.append(("up", 0, 8, 512 * i, n))
            pl.append(("up", 0, 8, D_FF + 512 * i, n))
        for cb_ in range(2):
            for kb, nr_ in ((0, 8), (1, 8), (2, 6)):
                pl.append(("down", 8 * kb, nr_, 512 * cb_, 512))
        return pl

    wbf = {}
    for key in ("in", "a", "b", "out", "up", "down"):
        rows, cols = W[key].shape
        t_ = nc.dram_tensor("wbf_" + key, [rows, cols], BF16, kind="Internal")
        p.reg["wbf_" + key] = Buf("wbf_" + key, t_)
        wbf[key] = t_.ap()

    ntiles = NPT + (1 if do_sample else 0)
    plan = tile_plan() * ntiles
    wstate = {"issued": 0, "cur": 0}

    REG = {"in": [(OFF_XBC, OFF_RW), (OFF_Z, OFF_XBC), (OFF_RW, OFF_G), (OFF_G, IN_DIM)],
           "a": [(0, 1024)], "b": [(0, 1024)], "out": [(0, 1024)], "up": [(0, D_FF), (D_FF, 2 * D_FF)],
           "down": [(0, 1024)]}
    cast_done = set()

    def ensure_cast(key, c0):
        for (lo, hi) in REG[key]:
            if lo <= c0 < hi:
                if (key, lo) not in cast_done:
                    cast_done.add((key, lo))
                    rows = W[key].shape[0]
                    step = 256 if (hi - lo) > 2048 else 512
                    for r_ in range(0, rows, step):
                        r1_ = min(rows, r_ + step)
                        p.dma("pool", wbf[key][r_:r1_, lo:hi], W[key][r_:r1_, lo:hi])
                return
        raise AssertionError((key, c0))

    def w_issue(upto):
        while wstate["issued"] < min(upto, len(plan)):
            i = wstate["issued"]
            key, r0_, nr_, c0, ncol = plan[i]
            ensure_cast(key, c0)
            s = slots[i % NS]
            src = wbf[key][r0_ * 128:(r0_ + nr_) * 128, c0:c0 + ncol].rearrange("(k p) n -> p k n", p=128)
            p.dma("pool", s[:, 0:nr_, 0:ncol], src)
            wstate["issued"] += 1

    def w_next(expect, issue=True):
        i = wstate["cur"]
        assert plan[i][0] == expect[0] and plan[i][3] == expect[1], (plan[i], expect)
        if issue:
            w_issue(i + NS)
        assert wstate["issued"] > i
        wstate["cur"] += 1
        return slots[i % NS]

    out_bufs = []

    def st(q, dram_ap, sb_ap, **kw):
        b = p.dma(q, dram_ap, sb_ap, **kw)
        if b not in out_bufs:
            out_bufs.append(b)

    def emit_rows(getc, nch, nrows, dram2d, stage):
        c = 0
        while c < nch:
            g0 = c
            while c < nch and c - g0 < 16:
                pb = p.bank()
                n4 = min(4, nch - c, 16 - (c - g0))
                for cc in range(n4):
                    p.mm(pb[0:nrows, cc * 128:(cc + 1) * 128], lhsT=getc(c + cc), rhs=ID)
                p.cp(stage[0:nrows, (c - g0) * 128:(c - g0 + n4) * 128], pb[0:nrows, 0:n4 * 128], eng="act")
                c += n4
            st("sp", dram2d[:, g0 * 128:c * 128], stage[0:nrows, 0:(c - g0) * 128])

    xloaded = {}
    def do_tile(ti, is_s):
        last_p = (not is_s) and ti == NPT - 1
        xr = xres[ti % 2]
        TRI = cst[:, (C_TRIS if is_s else C_TRIP):(C_TRIS if is_s else C_TRIP) + 128]
        SU = cst[:, (C_SUS if is_s else C_SUP):(C_SUS if is_s else C_SUP) + 128]
        UU = cst[:, (C_UUS if is_s else C_UUP):(C_UUS if is_s else C_UUP) + 128]
        MSK2 = cst[:, (C_SUS if is_s else C_SUP):(C_SUS if is_s else C_SUP) + 256]
        SCN = cst[:, C_RST:C_RST + 128] if is_s else ONES
        xsrc = xs if is_s else xp[ti * 128:(ti + 1) * 128, :]
        ydst = y_s if is_s else y_p[ti * 128:(ti + 1) * 128, :]

        def pv(c0, n, j, hist):
            if is_s:
                v = projT[:, c0:c0 + n, :].rearrange("p c (b t) -> p c b t", t=11)
                return v[:, :, :, 3 - hist + j:3 - hist + j + 8]
            return projT[:, c0:c0 + n, 3 - hist + j:3 - hist + j + 128]

        def tv(ap3):
            if is_s:
                return ap3.rearrange("p c (b t) -> p c b t", t=8)
            return ap3

        def bcp(ap2, n):
            if is_s:
                return ap2.unsqueeze(2).unsqueeze(3).to_broadcast([128, n, 16, 8])
            return ap2.unsqueeze(2).to_broadcast([128, n, 128])

        def rms_to_hT(src, gname):
            p.ms(sm_r[:, 0:1], 0.0)
            p.act(hb, src, AF.Square, accum=sm_r[:, 0:1])
            p.act(sm_r[:, 1:2], sm_r[:, 0:1], AF.Ln, bias=1e-5, scale=1.0 / 1024)
            p.act(sm_r[:, 1:2], sm_r[:, 1:2], AF.Exp, scale=-0.5)
            p.act(hb, src, AF.Copy, scale=sm_r[:, 1:2])
            pb = p.bank()
            pbv = pb[:].bitcast(BF16)
            for c in range(8):
                p.tr(pbv[:, c * 128:(c + 1) * 128], hb[:, c * 128:(c + 1) * 128], identb[:])
            p.tt(hT[:], pbv.rearrange("p (c t) -> p c t", t=128), bc(par(gname, 0, 8), [128, 8, 128], 2), ALU.mult)

        if not xloaded.get(ti):
            p.dma("sp", xr[:], xsrc)
        rms_to_hT(xr[:], "n1g")

        if is_s:
            sc2 = s_conv.rearrange("b t c -> (b t) c")
            for piece in range(3):
                p.dma("sp", rowsbuf[0:48, :], sc2[:, piece * 1024:(piece + 1) * 1024])
                pb = p.bank()
                for cc in range(8):
                    p.tr(pb[:, cc * 48:(cc + 1) * 48], rowsbuf[0:48, cc * 128:(cc + 1) * 128], cst[0:48, C_ID:C_ID + 48])
                dstv = projT[:, piece * 8:piece * 8 + 8, :].rearrange("p c (b t) -> p c b t", t=11)[:, :, :, 0:3]
                p.cp(dstv, pb[:, 0:384].rearrange("p (c b t) -> p c b t", c=8, b=16), eng="act")
        else:
            p.cp(projT[:, 0:24, 0:3], carry_conv[:])

        for i in range(6):
            s = w_next(("in", OFF_XBC + 512 * i))
            pb = p.bank()
            for cc in range(4):
                for kc in range(8):
                    p.mm(pb[:, cc * 128:(cc + 1) * 128], lhsT=s[:, kc, cc * 128:(cc + 1) * 128], rhs=hT[:, kc, :],
                         start=(kc == 0), stop=(kc == 7))
            p.cp(pv(4 * i, 4, 3, 3), tv(pb[:].rearrange("p (c t) -> p c t", t=128)), eng="act")
        if is_s:
            cmpb = BG[2]
            cv = cmpb[:, 0:24 * 48].rearrange("p (c b t) -> p c b t", c=24, b=16)
            p.cp(cv, projT[:, 0:24, :].rearrange("p c (b t) -> p c b t", t=11)[:, :, :, 8:11])
            emit_rows(lambda c: cmpb[:, c * 48:(c + 1) * 48], 24, 48, o_conv_s.rearrange("b t c -> (b t) c"), BG[3])
        else:
            p.cp(carry_conv[:], projT[:, 0:24, 128:131])
            if last_p:
                emit_rows(lambda c: carry_conv[:, c, :], 24, 3, o_conv_p, BG[3])

        nxt = ti + 1
        if nxt < ntiles:
            nsrc = xs if nxt == NPT else xp[nxt * 128:(nxt + 1) * 128, :]
            p.dma("sp", xres[nxt % 2][:], nsrc)
            xloaded[nxt] = True
        s = w_next(("in", OFF_DT))
        pb = p.bank()
        for kc in range(8):
            p.mm(pb[:, 0:32], lhsT=hT[:, kc, :], rhs=s[:, kc, 0:32], start=(kc == 0), stop=(kc == 7))
        dtv = sm_d[:, 0:32]
        dtA = sm_d[:, 32:64]
        ex = sm_d[:, 64:160]
        p.tt(dtv, pb[:, 0:32], pbc[:, PB_DTB:PB_DTB + 32], ALU.add)
        p.act(dtv, dtv, AF.Exp)
        p.act(dtv, dtv, AF.Ln, bias=1.0)
        p.tt(dtA, dtv, Ab[:], ALU.mult)
        pbk = p.bank()
        p.mm(pbk[:, 0:32], lhsT=TRI, rhs=dtA)
        p.mm(pbk[:, 32:64], lhsT=UU, rhs=dtA)
        p.mm(pbk[:, 64:96], lhsT=ONES, rhs=dtA)
        p.act(ex, pbk[:, 0:96], AF.Exp)
        expcs = sm_d[:, 64:96]
        toend = sm_d[:, 96:128]
        decb = sm_d[:, 128:160]

        def accv(c):
            return BG[0][:, c * 128:(c + 1) * 128] if c < 16 else BG[1][:, (c - 16) * 128:(c - 15) * 128]

        def thv(c0):
            return BG[2][:, c0 * 128:(c0 + 8) * 128] if c0 < 16 else BG[1][:, 1024:2048]

        for c in range(24):
            p.act(tv(accv(c).unsqueeze(1)), pv(c, 1, 0, 3), AF.Identity, scale=par("cw", c), bias=par("cb", c))
        for c in range(24):
            a_ = tv(accv(c).unsqueeze(1))
            for j in range(1, 4):
                p.stt(a_, pv(c, 1, j, 3), par("cw", j * 24 + c), a_, ALU.mult, ALU.add)
        for c0 in (0, 8, 16):
            a8 = BG[0][:, c0 * 128:(c0 + 8) * 128] if c0 < 16 else BG[1][:, 0:1024]
            p.act(thv(c0), a8, AF.Tanh)
            p.stt(xcT[:, c0:c0 + 8, :].rearrange("p c t -> p (c t)"), thv(c0), 1.0, a8, ALU.add, ALU.mult)

        for i in range(4):
            s = w_next(("in", OFF_Z + 512 * i))
            pb = p.bank()
            for kc in range(8):
                p.mm(pb[:], lhsT=hT[:, kc, :], rhs=s[:, kc, :], start=(kc == 0), stop=(kc == 7))
            th = BG[3][:, 0:512]
            p.act(th, pb[:], AF.Tanh, scale=0.5)
            p.stt(zs2[:, i * 512:(i + 1) * 512], th, 1.0, pb[:], ALU.add, ALU.mult)

        def rw_hist():
            if is_s:
                for piece in range(4):
                    ncol = 1024 if piece < 3 else 256
                    p.dma("sp", rowsbuf[0:16, 0:ncol], s_shift[:, piece * 1024:piece * 1024 + ncol])
                    nchk = ncol // 128
                    pb = p.bank()
                    for cc in range(nchk):
                        p.tr(pb[:, cc * 16:(cc + 1) * 16], rowsbuf[0:16, cc * 128:(cc + 1) * 128], cst[0:16, C_ID:C_ID + 16])
                    dstv = projT[:, piece * 8:piece * 8 + nchk, :].rearrange("p c (b t) -> p c b t", t=11)[:, :, :, 2:3]
                    p.cp(dstv, pb[:, 0:nchk * 16].rearrange("p (c b t) -> p c b t", c=nchk, b=16), eng="act")
            else:
                p.cp(projT[:, 0:26, 2:3], carry_shift[:])

        def rw_block(i):
            n4 = 4 if i < 6 else 2
            s = w_next(("in", OFF_RW + 512 * i))
            pb = p.bank()
            for cc in range(n4):
                for kc in range(8):
                    p.mm(pb[:, cc * 128:(cc + 1) * 128], lhsT=s[:, kc, cc * 128:(cc + 1) * 128], rhs=hT[:, kc, :],
                         start=(kc == 0), stop=(kc == 7))
            p.cp(pv(4 * i, n4, 1, 1), tv(pb[:, 0:n4 * 128].rearrange("p (c t) -> p c t", t=128)), eng="act")

        def rw_post():
            if is_s:
                cmpb = BG[2]
                cv = cmpb[:, 0:26 * 16].rearrange("p (c b t) -> p c b t", c=26, b=16)
                p.cp(cv, projT[:, 0:26, :].rearrange("p c (b t) -> p c b t", t=11)[:, :, :, 10:11])
                emit_rows(lambda c: cmpb[:, c * 16:(c + 1) * 16], 26, 16, o_shift_s, BG[3])
            else:
                p.cp(carry_shift[:], projT[:, 0:26, 130:131])
                if last_p:
                    emit_rows(lambda c: carry_shift[:, c, :], 26, 1, o_shift_p, BG[3])

        rw_state = {"n": 0}

        def rw_some(k):
            if rw_state["n"] == 0:
                rw_hist()
            for _ in range(k):
                if rw_state["n"] < 7:
                    rw_block(rw_state["n"])
                    rw_state["n"] += 1
                    if rw_state["n"] == 7:
                        rw_post()

        x_tok, xdt, xD, xdts = HB[0], HB[1], HB[2], HB[3]
        for half in range(2):
            pb = p.bank()
            pbv = pb[:].bitcast(BF16)
            for j in range(8):
                p.tr(pbv[:, j * 128:(j + 1) * 128], xcT[:, half * 8 + j, :], identb[:])
            p.cp(x_tok[:, half * 1024:(half + 1) * 1024], pbv, eng="act")
        pb = p.bank()
        pbv = pb[:].bitcast(BF16)
        for g in range(4):
            p.tr(pbv[:, g * 128:(g + 1) * 128], xcT[:, 16 + g, :], identb[:])
        p.cp(B_tok[:], pbv[:, 0:512], eng="act")
        x3 = x_tok[:].rearrange("p (h d) -> p h d", d=64)
        p.tt(xdt[:].rearrange("p (h d) -> p h d", d=64), x3, bc(dtv, [128, 32, 64], 2), ALU.mult)
        p.tt(xD[:].rearrange("p (h d) -> p h d", d=64), x3, bc(pbc[:, PB_D:PB_D + 32], [128, 32, 64], 2), ALU.mult)
        p.tt(xdts[:].rearrange("p (h d) -> p h d", d=64), xdt[:].rearrange("p (h d) -> p h d", d=64),
             bc(toend, [128, 32, 64], 2), ALU.mult)

        yoff = BG[3]
        if is_s:
            dtx = BG[1]
            p.cp(dtx[:].rearrange("p (h d) -> p h d", d=64), bc(dtA, [128, 32, 64], 2))
            pbd = p.bank()
            for jj in range(16):
                p.mm(pbd[:, jj * 16:(jj + 1) * 16], lhsT=dtx[:, jj * 128:(jj + 1) * 128], rhs=SEQ)
            decP = sm_p[:, 0:256]
            p.act(decP, pbd[:, 0:256], AF.Exp)
            decP3 = decP.rearrange("p (j b) -> p j b", b=16)
            p.ms(yoff[:], 0.0)
            h0Ts = [actT[:].rearrange("p c t -> p (c t)")[:, 0:2048], yaT.rearrange("p c t -> p (c t)")]
            Bms = [MTs[i][:].rearrange("p h l -> p (h l)")[:, 0:512] for i in range(2)]

            def seqgen(b):
                stin = BG[0] if b % 2 == 0 else BG[2]
                h0T = h0Ts[b % 2]
                Bm = Bms[b % 2]
                src = s_ssm[b].rearrange("(j q) d n -> (q d) j n", q=2)
                p.dma("sp", stin[:].rearrange("p (j n) -> p j n", n=128), src)
                for g4 in range(4):
                    pb = p.bank()
                    for jj in range(4):
                        j = g4 * 4 + jj
                        p.tr(pb[:, jj * 128:(jj + 1) * 128], stin[:, j * 128:(j + 1) * 128], ID)
                    p.cp(h0T[:, g4 * 512:(g4 + 1) * 512], pb[:], eng="act")
                yield
                for gq in range(4):
                    pb = p.bank()
                    p.mm(pb[:], lhsT=xcT[:, 20 + gq, :], rhs=h0T[:, gq * 512:(gq + 1) * 512])
                    p.stt(yoff[:, gq * 512:(gq + 1) * 512], pb[:], SEQ[:, b:b + 1], yoff[:, gq * 512:(gq + 1) * 512],
                          ALU.mult, ALU.add)
                yield
                p.ts(Bm, B_tok[:], SEQ[:, b:b + 1], ALU.mult)
                p.tt(stin[:].rearrange("p (j n) -> p j n", n=128), stin[:].rearrange("p (j n) -> p j n", n=128),
                     bc(decP3[:, :, b], [128, 16, 128], 2), ALU.mult)
                for g4 in range(4):
                    pb = p.bank()
                    for jj in range(4):
                        j = g4 * 4 + jj
                        p.mm(pb[:, jj * 128:(jj + 1) * 128], lhsT=xdts[:, j * 128:(j + 1) * 128],
                             rhs=Bm[:, (j // 4) * 128:(j // 4 + 1) * 128])
                    p.tt(stin[:, g4 * 512:(g4 + 1) * 512], stin[:, g4 * 512:(g4 + 1) * 512], pb[:], ALU.add)
                st("sp", o_ssm_s[b].rearrange("(j q) d n -> (q d) j n", q=2), stin[:].rearrange("p (j n) -> p j n", n=128))

            for b0_ in range(0, 16, 2):
                live_ = [seqgen(b0_), seqgen(b0_ + 1)]
                while live_:
                    for g_ in list(live_):
                        try:
                            next(g_)
                        except StopIteration:
                            live_.remove(g_)

        ya_all = BG[1]
        ssqg = sm_q[:, 0:4]
        p.ms(ssqg, 0.0)
        def grp(gq):
            par_ = gq % 2
            rb_ = BG[0] if (par_ == 0 or is_s) else BG[3]
            Rbuf = rb_[:].bitcast(BF16)[:, 0:1024]
            Mexp = rb_[:, 1024:2048]
            ytmp = BG[2][:, par_ * 512:(par_ + 1) * 512]
            sqj = BG[2][:, 1024 + par_ * 512:1024 + (par_ + 1) * 512]
            MT = MTs[par_]
            cbm = cbms[par_]
            p.tt(Rbuf.rearrange("p (h l) -> p h l", l=128), bc(dtA[:, 8 * gq:8 * gq + 8], [128, 8, 128], 2),
                 bc(TRI, [128, 8, 128], 1), ALU.mult)
            for hf in range(2):
                pS = p.bank()
                p.mm(pS[:], lhsT=UUb[:, 1 if is_s else 0, :], rhs=Rbuf[:, hf * 512:(hf + 1) * 512])
                p.act(Mexp[:, hf * 512:(hf + 1) * 512], pS[:], AF.Exp)
            pcb = p.bank()
            p.mm(pcb[:, 0:128], lhsT=xcT[:, 16 + gq, :], rhs=xcT[:, 20 + gq, :])
            yield
            p.tt(cbm[:], pcb[:, 0:128], TRI, ALU.mult)
            p.tt(MT[:], Mexp.rearrange("p (h l) -> p h l", l=128), bc(cbm[:], [128, 8, 128], 1), ALU.mult)
            yield
            pY = p.bank()
            for h in range(8):
                col = (8 * gq + h) * 64
                p.mm(pY[:, h * 64:(h + 1) * 64], lhsT=MT[:, h, :], rhs=xdt[:, col:col + 64], start=True, stop=False)
                p.mm(pY[:, h * 64:(h + 1) * 64], lhsT=identb[:], rhs=xD[:, col:col + 64], start=False, stop=True)
            if is_s:
                p.tt(ytmp.rearrange("p (h d) -> p h d", d=64),
                     yoff[:, gq * 512:(gq + 1) * 512].rearrange("p (h d) -> p h d", d=64),
                     bc(expcs[:, 8 * gq:8 * gq + 8], [128, 8, 64], 2), ALU.mult)
            else:
                pO = p.bank()
                p.mm(pO[:], lhsT=xcT[:, 20 + gq, :], rhs=hTbf[:, gq * 512:(gq + 1) * 512])
                p.tt(ytmp.rearrange("p (h d) -> p h d", d=64), pO[:].rearrange("p (h d) -> p h d", d=64),
                     bc(expcs[:, 8 * gq:8 * gq + 8], [128, 8, 64], 2), ALU.mult)
            if not is_s:
                rw_some(1)
            yield
            p.tt(ytmp, ytmp, pY[:], ALU.add)
            yag = ya_all[:, gq * 512:(gq + 1) * 512]
            p.stt(yag, ytmp, 0.5, zs2[:, gq * 512:(gq + 1) * 512], ALU.mult, ALU.mult)
            p.act(sqj, yag, AF.Square, accum=ssqg[:, gq:gq + 1])

        for g0_ in (0, 2):
            gens_ = [grp(g0_), grp(g0_ + 1)]
            if is_s:
                for g_ in gens_:
                    for _ in g_:
                        pass
            else:
                live_ = list(gens_)
                while live_:
                    for g_ in list(live_):
                        try:
                            next(g_)
                        except StopIteration:
                            live_.remove(g_)
        rstg = sm_q[:, 4:8]
        p.act(rstg, ssqg, AF.Ln, bias=1e-5, scale=1.0 / 512)
        p.act(rstg, rstg, AF.Exp, scale=-0.5)
        yn = HB[0]
        for gq in range(4):
            p.ts(yn[:, gq * 512:(gq + 1) * 512], ya_all[:, gq * 512:(gq + 1) * 512], rstg[:, gq:gq + 1], ALU.mult)
        for half in range(2):
            pb = p.bank()
            pbv = pb[:].bitcast(BF16)
            for j in range(8):
                p.tr(pbv[:, j * 128:(j + 1) * 128], yn[:, (half * 8 + j) * 128:(half * 8 + j + 1) * 128], identb[:])
            p.tt(yaT[:, half * 8:half * 8 + 8, :], pbv.rearrange("p (c t) -> p c t", t=128),
                 bc(par("ng", half * 8, 8), [128, 8, 128], 2), ALU.mult)
        if not is_s:
            for gq in range(4):
                pst = p.bank()
                p.mm(pst[:], lhsT=B_tok[:, gq * 128:(gq + 1) * 128], rhs=xdts[:, gq * 512:(gq + 1) * 512])
                hv = hst[:, gq * 512:(gq + 1) * 512]
                p.tt(hv.rearrange("p (h d) -> p h d", d=64), hv.rearrange("p (h d) -> p h d", d=64),
                     bc(decb[:, 8 * gq:8 * gq + 8], [128, 8, 64], 2), ALU.mult)
                p.tt(hv, hv, pst[:], ALU.add)
                p.cp(hTbf[:, gq * 512:(gq + 1) * 512], hv, eng="act")
            if last_p:
                stg = BG[0]
                for g4 in range(4):
                    pb = p.bank()
                    for jj in range(4):
                        j = g4 * 4 + jj
                        p.tr(pb[:, jj * 128:(jj + 1) * 128], hst[:, j * 128:(j + 1) * 128], ID)
                    p.cp(stg[:, g4 * 512:(g4 + 1) * 512], pb[:], eng="act")
                st("sp", o_ssm_p.rearrange("(j q) d n -> (q d) j n", q=2), stg[:].rearrange("p (j n) -> p j n", n=128))

        def do_ua():
            for cb_ in range(2):
                pb = p.bank()
                sw = [w_next(("a", 512 * cb_)), w_next(("a", 512 * cb_), issue=False)]
                for cc in range(4):
                    for kb in range(2):
                        for kc in range(8):
                            p.mm(pb[:, cc * 128:(cc + 1) * 128], lhsT=sw[kb][:, kc, cc * 128:(cc + 1) * 128],
                                 rhs=yaT[:, kb * 8 + kc, :], start=(kb == 0 and kc == 0), stop=(kb == 1 and kc == 7))
                p.cp(uaT[:, cb_ * 4:cb_ * 4 + 4, :], pb[:].rearrange("p (c t) -> p c t", t=128), eng="act")

        rw_some(7)

        def F(i):
            return BG[i // 2][:, (i % 2) * 1024:(i % 2 + 1) * 1024].rearrange("p (c t) -> p c t", t=128)

        rT, kT, vT, lw, aT, kkb, f6, f7 = [F(i) for i in range(8)]

        def mix(dst, c0, n):
            tmp = f7[:, 0:n, :]
            p.tt(tv(tmp), pv(c0, n, 0, 1), pv(c0, n, 1, 1), ALU.subtract)
            p.tt(tv(tmp), tv(tmp), bcp(par("mu", c0, n), n), ALU.mult)
            p.tt(tv(dst), tv(tmp), pv(c0, n, 1, 1), ALU.add)

        mix(rT, 0, 8)
        mix(kT, 8, 8)
        mix(vT, 16, 8)
        mix(lT[:], 24, 2)
        p.tt(kkb, kT, bc(par("kk", 0, 8), [128, 8, 128], 2), ALU.mult)
        sqb = HB[0][:, 0:1024]
        rkb = HB[0][:, 1024:2048]
        p.tt(sqb.rearrange("p (c t) -> p c t", t=128), kkb, kkb, ALU.mult)
        for hf in range(2):
            pq = p.bank()
            p.mm(pq[:], lhsT=BLKb[:], rhs=sqb[:, hf * 512:(hf + 1) * 512])
            p.ts(f6[:, hf * 4:hf * 4 + 4, :], pq[:].rearrange("p (c t) -> p c t", t=128), 1e-24, ALU.max)
        p.act(f6, f6, AF.Ln)
        p.act(f6, f6, AF.Exp, scale=-0.5)
        p.tt(kkb, kkb, f6, ALU.mult)
        p.ms(lb[:, 0, :], 0.0)
        p.ms(lb[:, 2, :], 0.0)
        p.act(lb[0:64, 0, :], lT[0:64, 0, :], AF.Tanh)
        p.cp(lb[64:128, 2, :], lT[64:128, 0, :])
        p.act(lT[:, 1, :], lT[:, 1, :], AF.Tanh, scale=0.5)
        p.ts(lb[:, 1, :], lT[:, 1, :], 0.5, ALU.mult, 0.5, ALU.add)
        for c in range(8):
            pL = p.bank()
            p.mm(pL[:, 0:128], lhsT=wa_up[:, c * 128:(c + 1) * 128], rhs=lb[:, 0, :])
            p.mm(pL[:, 128:256], lhsT=wa_up[:, c * 128:(c + 1) * 128], rhs=lb[:, 2, :])
            p.mm(pL[:, 256:384], lhsT=g_up[:, c * 128:(c + 1) * 128], rhs=lb[:, 1, :])
            p.act(lw[:, c, :], pL[:, 0:128], AF.Tanh, scale=0.5, bias=par("w0", c))
            p.act(aT[:, c, :], pL[:, 128:256], AF.Tanh, scale=0.5, bias=par("a0", c))
            p.cp(G16[:, c, :], pL[:, 256:384], eng="act")
        do_ua()
        CW = -0.5 * float(np.exp(-0.5))
        p.ts(lw, lw, CW, ALU.mult, CW, ALU.add)
        p.ts(aT, aT, 0.5, ALU.mult, 0.5, ALU.add)
        p.stt(f6, aT, -1.0, bc(par("ka", 0, 8), [128, 8, 128], 2), ALU.add, ALU.mult)
        p.stt(kT, f6, 1.0, kT, ALU.add, ALU.mult)
        p.tt(f6, rT, kT, ALU.mult)
        p.tt(rkb.rearrange("p (c t) -> p c t", t=128), f6, bc(par("rk", 0, 8), [128, 8, 128], 2), ALU.mult)
        vbf = HB[2][:, 0:1024].rearrange("p (c t) -> p c t", t=128)
        p.cp(vbf, vT)
        for hf in range(2):
            pq = p.bank()
            p.mm(pq[:], lhsT=BLKb[:], rhs=rkb[:, hf * 512:(hf + 1) * 512])
            p.tt(f6[:, hf * 4:hf * 4 + 4, :], pq[:].rearrange("p (c t) -> p c t", t=128), vT[:, hf * 4:hf * 4 + 4, :], ALU.mult)
        bv = f6
        p.tt(aT, kkb, aT, ALU.mult)
        csT = vT
        for c in range(8):
            p.scan(csT[:, c, :], SCN, lw[:, c, :])
        AR = HB[0][:].rearrange("p (c a t) -> p c a t", a=2, t=128)
        bT = HB[1][:, 0:1024].rearrange("p (c t) -> p c t", t=128)
        kT2 = HB[1][:, 1024:2048].rearrange("p (c t) -> p c t", t=128)
        p.tt(lw, csT, lw, ALU.subtract)
        p.act(lw, lw, AF.Exp)
        p.stt(AR[:, :, 0, :], kkb, -1.0, lw, ALU.mult, ALU.mult)
        p.act(lw, csT, AF.Exp)
        p.tt(AR[:, :, 1, :], rT, lw, ALU.mult)
        glast = sm_g[:, 0:128]
        if is_s:
            p.cp(glast.rearrange("p (c b) -> p c b", b=16), lw.rearrange("p c (b t) -> p c b t", t=8)[:, :, :, 7])
        else:
            p.cp(glast[:, 0:8], lw[:, :, 127])
        p.act(lw, csT, AF.Exp, scale=-1.0)
        p.tt(bT, aT, lw, ALU.mult)
        p.tt(kT2, kT, lw, ALU.mult)
        V_tok = HB[2][:, 1024:2048]
        b_tok = HB[3][:, 0:1024]
        k_tok = HB[3][:, 1024:2048]
        for src3, dst in ((vbf, V_tok), (bT, b_tok), (kT2, k_tok)):
            pb = p.bank()
            pbv = pb[:].bitcast(BF16)
            for c in range(8):
                p.tr(pbv[:, c * 128:(c + 1) * 128], src3[:, c, :], identb[:])
            p.cp(dst, pbv, eng="act")

        ysb = F(0)

        def heads(bi):
            return [(2 * bi + jj, q) for q in range(2) for jj in range(2)]

        def abc(bi):
            k = bi % 2
            j0 = 2 * bi
            NY, AY = NYb[k], AYb[k]
            pa = [p.bank(), p.bank()]
            pbk_ = [p.bank(), p.bank()]
            pc = [p.bank(), p.bank()]
            for hh, (j, q) in enumerate(heads(bi)):
                jj = j - j0
                qs = slice(64 * q, 64 * q + 64)
                arr = AR[qs, j, :, :].rearrange("p a t -> p (a t)")
                p.mm(pa[q][:, jj * 256:(jj + 1) * 256], lhsT=bT[qs, j, :], rhs=arr)
                p.mm(pbk_[q][:, jj * 256:(jj + 1) * 256], lhsT=kT2[qs, j, :], rhs=arr)
                p.mm(pc[q][:, jj * 128:(jj + 1) * 128], lhsT=AR[qs, j, 0, :], rhs=bT[qs, j, :])
            m2 = bc(MSK2.rearrange("p (a t) -> p a t", a=2), [128, 2, 2, 128], 1)
            for q in range(2):
                p.tt(NY[:, 2 * q:2 * q + 2, :, :], pa[q][:].rearrange("p (h a t) -> p h a t", h=2, a=2), m2, ALU.mult)
                p.tt(AY[:, 2 * q:2 * q + 2, :, :], pbk_[q][:].rearrange("p (h a t) -> p h a t", h=2, a=2), m2, ALU.mult)
                p.tt(Mq[k][0][:, 2 * q:2 * q + 2, :], pc[q][:, 0:256].rearrange("p (h t) -> p h t", t=128),
                     bc(UU, [128, 2, 128], 1), ALU.mult)
            p.cp(Nq[k][0][:], NY[:, :, 0, :])
            p.tt(Rb[k][:], NY[:, :, 0, :], bc(ID, [128, 4, 128], 1), ALU.add)

        def inv_step(bi, i):
            k = bi % 2
            Nc, Mc = Nq[k][i % 2], Mq[k][i % 2]
            Nn, Mn = Nq[k][(i + 1) % 2], Mq[k][(i + 1) % 2]
            if i >= 1:
                p1 = p.bank()
                for hh in range(4):
                    p.mm(p1[:, hh * 128:(hh + 1) * 128], lhsT=Mc[:, hh, :], rhs=Rb[k][:, hh, :])
            if i <= 4:
                p2 = p.bank()
                for hh in range(4):
                    p.mm(p2[:, hh * 128:(hh + 1) * 128], lhsT=Mc[:, hh, :], rhs=Nc[:, hh, :])
            if i <= 5:
                p3 = p.bank()
                for hh in range(4):
                    p.mm(p3[:, hh * 128:(hh + 1) * 128], lhsT=Nc[:, hh, :], rhs=Mc[:, hh, :])
            if i >= 1:
                p.tt(Rb[k][:], Rb[k][:], p1[:].rearrange("p (h t) -> p h t", t=128), ALU.add)
            if i <= 4:
                p.cp(Nn[:], p2[:].rearrange("p (h t) -> p h t", t=128), eng="act")
            if i <= 5:
                p.cp(Mn[:], p3[:].rearrange("p (h t) -> p h t", t=128), eng="act")

        def finish(bi):
            k = bi % 2
            W_sb, U_sb = W_sbs[k], U_sbs[k]
            NY, AY = NYb[k], AYb[k]
            j0 = 2 * bi
            hs = heads(bi)
            if is_s:
                Sin = BG[2]
                Hall = BG[1]
                Hallb = zs2
                S4 = Sin[:].rearrange("p (b j k) -> p b j k", b=16, j=2)
                H4 = Hall[:].rearrange("p (b j v) -> p b j v", b=16, j=2)
                Hb4 = Hallb[:].rearrange("p (b j v) -> p b j v", b=16, j=2)
                for jj_ in range(2):
                    p.dma("sp", S4[:, :, jj_, :], s_rwkv.rearrange("b (j q) v k -> (q v) b j k", q=2)[:, :, j0 + jj_, :])
                for g in range(4):
                    pbq = [p.bank(), p.bank()]
                    for e8 in range(8):
                        b, jj = divmod(g * 8 + e8, 2)
                        for q in range(2):
                            qs = slice(64 * q, 64 * q + 64)
                            p.mm(pbq[q][qs, e8 * 64:(e8 + 1) * 64], lhsT=S4[qs, b, jj, :],
                                 rhs=cst[qs, C_ID + 64 * q:C_ID + 64 * q + 64])
                    for q in range(2):
                        qs = slice(64 * q, 64 * q + 64)
                        p.cp(Hall[qs, g * 512:(g + 1) * 512], pbq[q][qs, :], eng="act")
                p.cp(Hallb[:], Hall[:])
                pw1 = [p.bank(), p.bank()]
                for hh, (j, q) in enumerate(hs):
                    qs = slice(64 * q, 64 * q + 64)
                    jj = j - j0
                    for b in range(16):
                        p.mm(pw1[q][qs, jj * 128 + 8 * b:jj * 128 + 8 * b + 8], lhsT=Hb4[qs, b, jj, :],
                             rhs=AR[qs, j, 0, 8 * b:8 * b + 8])
                for q in range(2):
                    qs = slice(64 * q, 64 * q + 64)
                    p.cp(W1T[qs, :, :], pw1[q][qs, 0:256].rearrange("p (j t) -> p j t", t=128), eng="act")
            pW = [p.bank(), p.bank()]
            for hh, (j, q) in enumerate(hs):
                qs = slice(64 * q, 64 * q + 64)
                hc = slice((2 * j + q) * 64, (2 * j + q) * 64 + 64)
                jj = j - j0
                o = pW[q][:, jj * 64:(jj + 1) * 64]
                if is_s:
                    p.mm(o, lhsT=W1T[qs, jj, :], rhs=identb[qs, 64 * q:64 * q + 64], start=True, stop=False)
                else:
                    p.mm(o, lhsT=AR[qs, j, 0, :], rhs=Hbf[qs, j, :], start=True, stop=False)
                p.mm(o, lhsT=AY[:, hh, 0, :], rhs=V_tok[:, hc], start=False, stop=True)
            for q in range(2):
                p.cp(W_sb[:, 2 * q:2 * q + 2, :], pW[q][:, 0:128].rearrange("p (h v) -> p h v", v=64), eng="act")
            yield
            pU = p.bank()
            for hh in range(4):
                p.mm(pU[:, hh * 64:(hh + 1) * 64], lhsT=Rb[k][:, hh, :], rhs=W_sb[:, hh, :])
            p.cp(U_sb[:], pU[:, 0:256].rearrange("p (h v) -> p h v", v=64), eng="act")
            yield
            pYo = [p.bank(), p.bank()]
            for hh, (j, q) in enumerate(hs):
                qs = slice(64 * q, 64 * q + 64)
                hc = slice((2 * j + q) * 64, (2 * j + q) * 64 + 64)
                jj = j - j0
                o = pYo[q][qs, jj * 128:(jj + 1) * 128]
                p.mm(o, lhsT=U_sb[:, hh, :], rhs=NY[:, hh, 1, :], start=True, stop=False)
                p.mm(o, lhsT=V_tok[:, hc], rhs=AY[:, hh, 1, :], start=False, stop=False)
                if is_s:
                    for b in range(16):
                        p.mm(pYo[q][qs, jj * 128 + 8 * b:jj * 128 + 8 * b + 8], lhsT=Hb4[qs, b, jj, :],
                             rhs=AR[qs, j, 1, 8 * b:8 * b + 8], start=False, stop=(b == 15))
                else:
                    p.mm(o, lhsT=Hbf[qs, j, :], rhs=AR[qs, j, 1, :], start=False, stop=True)
            for q in range(2):
                qs = slice(64 * q, 64 * q + 64)
                p.cp(ysb[qs, j0:j0 + 2, :], pYo[q][qs, 0:256].rearrange("p (j t) -> p j t", t=128), eng="act")
            if is_s:
                Ublk = yaT[:].rearrange("p c t -> p (c t)")[:, 0:1024].rearrange("p (b v) -> p b v", v=64)
                Vblk = yaT[:].rearrange("p c t -> p (c t)")[:, 1024:2048].rearrange("p (b v) -> p b v", v=64)
                sq3 = bc(SEQ, [128, 16, 64], 2)
                for jj in range(2):
                    pd = [p.bank(), p.bank()]
                    for q in range(2):
                        hh = 2 * q + jj
                        j = j0 + jj
                        qs = slice(64 * q, 64 * q + 64)
                        hc = slice((2 * j + q) * 64, (2 * j + q) * 64 + 64)
                        p.tt(Ublk, bc(U_sb[:, hh, :], [128, 16, 64], 1), sq3, ALU.mult)
                        p.tt(Vblk, bc(V_tok[:, hc], [128, 16, 64], 1), sq3, ALU.mult)
                        for hf in range(2):
                            p.mm(pd[hf][qs, :], lhsT=b_tok[:, hc], rhs=Ublk[:, hf * 8:hf * 8 + 8, :].rearrange("p b v -> p (b v)"),
                                 start=True, stop=False)
                            p.mm(pd[hf][qs, :], lhsT=k_tok[:, hc], rhs=Vblk[:, hf * 8:hf * 8 + 8, :].rearrange("p b v -> p (b v)"),
                                 start=False, stop=True)
                    for hf in range(2):
                        hv = H4[:, hf * 8:hf * 8 + 8, jj, :]
                        p.tt(hv, hv, pd[hf][:].rearrange("p (b v) -> p b v", v=64), ALU.add)
                    g3 = glast.rearrange("p (c b) -> p c b", b=16)[:, j0 + jj, :]
                    p.tt(H4[:, :, jj, :], H4[:, :, jj, :], bc(g3, [128, 16, 64], 2), ALU.mult)
                for g in range(4):
                    pbq = [p.bank(), p.bank()]
                    for e8 in range(8):
                        b, jj = divmod(g * 8 + e8, 2)
                        for q in range(2):
                            qs = slice(64 * q, 64 * q + 64)
                            p.mm(pbq[q][qs, e8 * 64:(e8 + 1) * 64], lhsT=H4[qs, b, jj, :],
                                 rhs=cst[qs, C_ID + 64 * q:C_ID + 64 * q + 64])
                    for q in range(2):
                        qs = slice(64 * q, 64 * q + 64)
                        p.cp(Sin[qs, g * 512:(g + 1) * 512], pbq[q][qs, :], eng="act")
                for jj_ in range(2):
                    st("sp", o_rwkv_s.rearrange("b (j q) v k -> (q v) b j k", q=2)[:, :, j0 + jj_, :], S4[:, :, jj_, :])
            else:
                pD = p.bank()
                for hh, (j, q) in enumerate(hs):
                    qs = slice(64 * q, 64 * q + 64)
                    hc = slice((2 * j + q) * 64, (2 * j + q) * 64 + 64)
                    jj = j - j0
                    p.mm(pD[qs, jj * 64:(jj + 1) * 64], lhsT=b_tok[:, hc], rhs=U_sb[:, hh, :], start=True, stop=False)
                    p.mm(pD[qs, jj * 64:(jj + 1) * 64], lhsT=k_tok[:, hc], rhs=V_tok[:, hc], start=False, stop=True)
                hv = Hst[:, j0:j0 + 2, :]
                p.tt(hv, hv, pD[:, 0:128].rearrange("p (j v) -> p j v", v=64), ALU.add)
                p.tt(hv, hv, bc(glast[:, j0:j0 + 2], [128, 2, 64], 2), ALU.mult)
                p.cp(Hbf[:, j0:j0 + 2, :], hv, eng="act")

        for pair in range(2):
            b0, b1 = 2 * pair, 2 * pair + 1
            abc(b0)
            abc(b1)
            for i in range(7):
                inv_step(b0, i)
                inv_step(b1, i)
            gens = [finish(b0), finish(b1)]
            if is_s:
                for g_ in gens:
                    for _ in g_:
                        pass
            else:
                live = list(gens)
                while live:
                    for g_ in list(live):
                        try:
                            next(g_)
                        except StopIteration:
                            live.remove(g_)

        if last_p:
            pbq = [p.bank(), p.bank()]
            for j in range(8):
                for q in range(2):
                    qs = slice(64 * q, 64 * q + 64)
                    p.mm(pbq[q][qs, j * 64:(j + 1) * 64], lhsT=Hst[qs, j, :], rhs=cst[qs, C_ID + 64 * q:C_ID + 64 * q + 64])
            stg = BG[2][:, 0:512]
            for q in range(2):
                qs = slice(64 * q, 64 * q + 64)
                p.cp(stg[qs, :], pbq[q][qs, :], eng="act")
            st("sp", o_rwkv_p.rearrange("(j q) v k -> (q v) j k", q=2), stg.rearrange("p (j k) -> p j k", k=64))

        s1, s2 = F(1), F(2)
        y2b = HB[0][:, 0:1024]
        p.tt(y2b.rearrange("p (c t) -> p c t", t=128), ysb, ysb, ALU.mult)
        for hf in range(2):
            pm_ = p.bank()
            pq = p.bank()
            p.mm(pm_[:], lhsT=BLK, rhs=ysb[:, hf * 4:hf * 4 + 4, :].rearrange("p c t -> p (c t)"))
            p.mm(pq[:], lhsT=BLKb[:], rhs=y2b[:, hf * 512:(hf + 1) * 512])
            p.ts(s1[:, hf * 4:hf * 4 + 4, :], pm_[:].rearrange("p (c t) -> p c t", t=128), 1.0 / 64, ALU.mult)
            p.ts(s2[:, hf * 4:hf * 4 + 4, :], pq[:].rearrange("p (c t) -> p c t", t=128), 1.0 / 64, ALU.mult)
        f3 = F(3)
        p.tt(f3, s1, s1, ALU.mult)
        p.tt(s2, s2, f3, ALU.subtract)
        p.act(s2, s2, AF.Ln, bias=64e-5)
        p.act(s2, s2, AF.Exp, scale=-0.5)
        p.tt(ysb, ysb, s1, ALU.subtract)
        p.tt(ysb, ysb, s2, ALU.mult)
        p.tt(ysb, ysb, bc(par("lnw", 0, 8), [128, 8, 128], 2), ALU.mult)
        p.tt(ysb, ysb, bc(par("lnb", 0, 8), [128, 8, 128], 2), ALU.add)
        p.tt(ysb, ysb, bv, ALU.add)
        p.tt(ybT[:], ysb, G16[:], ALU.mult)

        for cb_ in range(2):
            s = w_next(("b", 512 * cb_))
            pb = p.bank()
            for cc in range(4):
                for kc in range(8):
                    p.mm(pb[:, cc * 128:(cc + 1) * 128], lhsT=s[:, kc, cc * 128:(cc + 1) * 128], rhs=ybT[:, kc, :],
                         start=(kc == 0), stop=(kc == 7))
            p.cp(ubT[:, cb_ * 4:cb_ * 4 + 4, :], pb[:].rearrange("p (c t) -> p c t", t=128), eng="act")

        mf = F(4)
        for i in range(4):
            s = w_next(("in", OFF_G + 512 * i))
            pb = p.bank()
            for cc in range(4):
                for kc in range(8):
                    p.mm(pb[:, cc * 128:(cc + 1) * 128], lhsT=s[:, kc, cc * 128:(cc + 1) * 128], rhs=hT[:, kc, :],
                         start=(kc == 0), stop=(kc == 7))
            th = F(5)[:, 0:4, :]
            p.act(th, pb[:].rearrange("p (c t) -> p c t", t=128), AF.Tanh, scale=0.5)
            cs_ = slice((i % 2) * 4, (i % 2) * 4 + 4)
            if i < 2:
                p.stt(mf[:, cs_, :], th, 1.0, uaT[:, cs_, :], ALU.add, ALU.mult)
            else:
                p.stt(th, th, 1.0, ubT[:, cs_, :], ALU.add, ALU.mult)
                p.tt(mf[:, cs_, :], mf[:, cs_, :], th, ALU.add)
                p.ts(mT[:, cs_, :], mf[:, cs_, :], 0.5, ALU.mult)

        for cb_ in range(2):
            s = w_next(("out", 512 * cb_))
            pb = p.bank()
            for kc in range(8):
                p.mm(pb[:], lhsT=mT[:, kc, :], rhs=s[:, kc, :], start=(kc == 0), stop=(kc == 7))
            p.tt(xr[:, cb_ * 512:(cb_ + 1) * 512], xr[:, cb_ * 512:(cb_ + 1) * 512], pb[:], ALU.add)

        rms_to_hT(xr[:], "n2g")

        def uv_(ub, n, j):
            if is_s:
                return ub[:, 0:n, :].rearrange("p c (b t) -> p c b t", t=10)[:, :, :, j:j + 8]
            return ub[:, 0:n, j:j + 128]

        def ffn_blk(i):
            n4 = 4 if i < 5 else 2
            ub = ugblk[i % 2]
            s = w_next(("up", 512 * i))
            if is_s:
                ncol = n4 * 128
                p.dma("sp", rowsbuf[0:32, 0:ncol], s_ffn.rearrange("b t c -> (b t) c")[:, i * 512:i * 512 + ncol])
                pbh = p.bank()
                for cc in range(n4):
                    p.tr(pbh[:, cc * 32:(cc + 1) * 32], rowsbuf[0:32, cc * 128:(cc + 1) * 128], cst[0:32, C_ID:C_ID + 32])
                p.cp(uv_(ub, n4, 0)[:, :, :, 0:2], pbh[:, 0:n4 * 32].rearrange("p (c b t) -> p c b t", c=n4, b=16), eng="act")
            else:
                p.cp(ub[:, 0:n4, 0:2], carry_ffn[:, 4 * i:4 * i + n4, :])
            pb = p.bank()
            for cc in range(n4):
                for kc in range(8):
                    p.mm(pb[:, cc * 128:(cc + 1) * 128], lhsT=s[:, kc, cc * 128:(cc + 1) * 128], rhs=hT[:, kc, :],
                         start=(kc == 0), stop=(kc == 7))
            p.cp(uv_(ub, n4, 2), tv(pb[:, 0:n4 * 128].rearrange("p (c t) -> p c t", t=128)), eng="act")
            if is_s:
                cmpb = BG[2]
                cv = cmpb[:, 0:n4 * 32].rearrange("p (c b t) -> p c b t", c=n4, b=16)
                p.cp(cv, ub[:, 0:n4, :].rearrange("p c (b t) -> p c b t", t=10)[:, :, :, 8:10])
                emit_rows(lambda c: cmpb[:, c * 32:(c + 1) * 32], n4, 32,
                          o_ffn_s.rearrange("b t c -> (b t) c")[:, i * 512:i * 512 + n4 * 128], BG[3])
            else:
                p.cp(carry_ffn[:, 4 * i:4 * i + n4, :], ub[:, 0:n4, 128:130])
            s2_ = w_next(("up", D_FF + 512 * i))
            pv2 = p.bank()
            for cc in range(n4):
                for kc in range(8):
                    p.mm(pv2[:, cc * 128:(cc + 1) * 128], lhsT=s2_[:, kc, cc * 128:(cc + 1) * 128], rhs=hT[:, kc, :],
                         start=(kc == 0), stop=(kc == 7))
            odd_ = (i % 2 == 1) and not is_s
            acc_ = F(4) if odd_ else F(0)
            th_ = F(5) if odd_ else F(1)
            yield
            for cc in range(n4):
                c = 4 * i + cc
                a3 = tv(acc_[:, cc:cc + 1, :])
                p.act(a3, uv_(ub, n4, 0)[:, cc:cc + 1] if is_s else ub[:, cc:cc + 1, 0:128], AF.Identity,
                      scale=par("fw", c), bias=par("fb", c))
                for j in (1, 2):
                    src = uv_(ub, n4, j)[:, cc:cc + 1] if is_s else ub[:, cc:cc + 1, j:j + 128]
                    p.stt(a3, src, par("fw", j * 22 + c), a3, ALU.mult, ALU.add)
            yield
            p.act(th_[:, 0:n4, :], acc_[:, 0:n4, :], AF.Tanh)
            p.stt(acc_[:, 0:n4, :], th_[:, 0:n4, :], 1.0, acc_[:, 0:n4, :], ALU.add, ALU.mult)
            p.tt(actT[:, 4 * i:4 * i + n4, :], acc_[:, 0:n4, :], pv2[:, 0:n4 * 128].rearrange("p (c t) -> p c t", t=128), ALU.mult)

        for i0_ in (0, 2, 4):
            gens_ = [ffn_blk(i0_), ffn_blk(i0_ + 1)]
            if is_s:
                for g_ in gens_:
                    for _ in g_:
                        pass
            else:
                live_ = list(gens_)
                while live_:
                    for g_ in list(live_):
                        try:
                            next(g_)
                        except StopIteration:
                            live_.remove(g_)
        if last_p:
            emit_rows(lambda c: carry_ffn[:, c, :], 22, 2, o_ffn_p, BG[3])

        for cb_ in range(2):
            pb = p.bank()
            for kb, nr_ in ((0, 8), (1, 8), (2, 6)):
                s = w_next(("down", 512 * cb_))
                for kc in range(nr_):
                    p.mm(pb[:], lhsT=actT[:, kb * 8 + kc, :], rhs=s[:, kc, :], start=(kb == 0 and kc == 0),
                         stop=(kb == 2 and kc == nr_ - 1))
            p.tt(xr[:, cb_ * 512:(cb_ + 1) * 512], xr[:, cb_ * 512:(cb_ + 1) * 512], pb[:], ALU.add)
        p.ms(sm_r[:, 2:3], 0.0)
        p.act(hb, xr[:], AF.Square, accum=sm_r[:, 2:3])
        p.act(sm_r[:, 3:4], sm_r[:, 2:3], AF.Ln, bias=1e-5, scale=1.0 / 1024)
        p.act(sm_r[:, 3:4], sm_r[:, 3:4], AF.Exp, scale=-0.5)
        p.stt(xr[:], xr[:], sm_r[:, 3:4], pbc[:, PB_FG:PB_FG + 1024], ALU.mult, ALU.mult)
        st("sp", ydst, xr[:])

    for ti in range(NPT):
        do_tile(ti, False)
    if do_sample:
        do_tile(NPT, True)
    p.final_wait("sp", out_bufs)
    p.emit()
    return nc


def make_consts():
    r = np.arange(128)[:, None]
    c = np.arange(128)[None, :]
    same = (r // 8) == (c // 8)
    m = np.zeros((128, NCST), np.float32)
    m[:, C_ID:C_ID + 128] = (r == c)
    m[:, C_SUP:C_SUP + 128] = (r < c)
    m[:, C_TRIP:C_TRIP + 128] = (r <= c)
    m[:, C_UUP:C_UUP + 128] = (r > c)
    m[:, C_SUS:C_SUS + 128] = (r < c) & same
    m[:, C_TRIS:C_TRIS + 128] = (r <= c) & same
    m[:, C_UUS:C_UUS + 128] = (r > c) & same
    m[:, C_ONES:C_ONES + 128] = 1.0
    m[:, C_BLK:C_BLK + 128] = (r // 64) == (c // 64)
    m[:, C_RST:C_RST + 128] = (c % 8 != 0)
    m[:, C_SEQ:C_SEQ + 16] = (r // 8) == np.arange(16)[None, :]
    return m


def make_inputs(core, I):
    f = lambda a: np.ascontiguousarray(a, dtype=np.float32)
    rows = [I["norm1_g"][0].reshape(8, 128), I["norm2_g"][0].reshape(8, 128),
            I["ssm_conv_w"][0].reshape(4 * 24, 128), I["ssm_conv_b"][0].reshape(24, 128),
            I["rwkv_mu"][0].reshape(26, 128), I["ffn_conv_w"][0].reshape(3 * 22, 128),
            I["ffn_conv_b"][0].reshape(22, 128), I["rwkv_w0"][0].reshape(8, 128), I["rwkv_a0"][0].reshape(8, 128),
            I["rwkv_k_k"][0].reshape(8, 128), I["rwkv_k_a"][0].reshape(8, 128), I["rwkv_r_k"][0].reshape(8, 128),
            I["rwkv_ln_w"][0].reshape(8, 128), I["rwkv_ln_b"][0].reshape(8, 128), I["ssm_norm_g"][0].reshape(16, 128)]
    pbcv = np.concatenate([I["ssm_dt_bias"][0], I["ssm_a_log"][0], I["ssm_d"][0], I["final_g"]])
    sl = slice(16 * core, 16 * core + 16)
    return {
        "xp": f(I["x_prompt"][core]), "xs": f(I["x_sample"][sl].reshape(128, 1024)),
        "s_conv": f(I["state_ssm_conv"][0, sl]), "s_ssm": f(I["state_ssm"][0, sl]),
        "s_shift": f(I["state_rwkv_shift"][0, sl]), "s_rwkv": f(I["state_rwkv"][0, sl]),
        "s_ffn": f(I["state_ffn_conv"][0, sl]),
        "w_in": f(I["w_in"][0]), "w_a": f(I["w_branch_a"][0]), "w_b": f(I["w_branch_b"][0]),
        "w_out": f(I["w_out"][0]), "w_up": f(I["ffn_w_up"][0]), "w_down": f(I["ffn_w_down"][0]),
        "lora_wa": f(np.concatenate([I["rwkv_w_up"][0], I["rwkv_a_up"][0]], 0)), "lora_g": f(I["rwkv_g_up"][0]),
        "prow": f(np.concatenate(rows, 0)), "pbc_in": f(pbcv), "cst_in": make_consts(),
    }


def kernel(**I):
    I = {k: np.asarray(v) for k, v in I.items()}
    nc = build_nc()
    in_maps = [make_inputs(c, I) for c in range(NCORES)]
    res = run_bass_kernel_spmd(nc, in_maps, core_ids=list(range(NCORES)))
    R = res.results
    g = lambda k: np.stack([np.asarray(R[c][k], dtype=np.float32) for c in range(NCORES)], 0)
    cat = lambda k: np.concatenate([np.asarray(R[c][k], dtype=np.float32) for c in range(NCORES)], 0)
    return (g("y_p"), cat("y_s").reshape(128, 8, 1024),
            g("o_conv_p")[None], g("o_ssm_p")[None], g("o_shift_p").reshape(1, 8, 3328), g("o_rwkv_p")[None],
            g("o_ffn_p")[None],
            cat("o_conv_s")[None], cat("o_ssm_s")[None], cat("o_shift_s")[None], cat("o_rwkv_s")[None],
            cat("o_ffn_s")[None])
```

```python
import contextlib
import numpy as np
import concourse.bass as bass
import concourse.mybir as mybir
from concourse.bass_utils import run_bass_kernel_spmd

F32 = mybir.dt.float32
BF16 = mybir.dt.bfloat16
AF = mybir.ActivationFunctionType
ALU = mybir.AluOpType

NCORES = 8
OFF_Z, OFF_XBC, OFF_DT, OFF_RW, OFF_G = 0, 2048, 5120, 5152, 8480
IN_DIM = 10528
D_FF = 2816
C_ID, C_SUP, C_TRIP, C_UUP, C_SUS, C_TRIS, C_UUS, C_ONES, C_BLK, C_RST, C_SEQ = [128 * i for i in range(11)]
NCST = C_SEQ + 16
PR = {}
_o = 0
for _n, _r in (("n1g", 8), ("n2g", 8), ("cw", 96), ("cb", 24), ("mu", 26), ("fw", 66), ("fb", 22),
               ("w0", 8), ("a0", 8), ("kk", 8), ("ka", 8), ("rk", 8), ("lnw", 8), ("lnb", 8), ("ng", 16)):
    PR[_n] = _o
    _o += _r
NPR = _o
PB_DTB, PB_ALOG, PB_D, PB_FG = 0, 32, 64, 96
NPB = 96 + 1024


class Buf:
    def __init__(self, name, t):
        self.name = name
        self.t = t
        self.lw = None
        self.rd = []
        self.dkey = None
        self.dcnt = 0

    def __getitem__(self, k):
        return self.t[k]


class Prog:
    ENGS = ["pe", "act", "dve", "pool", "sp"]
    STRICT_ENGS = ("act", "dve", "pool")

    def __init__(self, nc):
        self.nc = nc
        self.es = contextlib.ExitStack()
        self.ops = {e: [] for e in self.ENGS}
        self.cnt = {e: 0 for e in self.ENGS}
        self.sems = {}
        self.reg = {}
        for e in self.ENGS:
            self.sems[e] = self.es.enter_context(nc.semaphore("s_" + e))
        self.banks = []
        self.bi = 0

    def sb(self, name, shape, dt=F32):
        t = self.es.enter_context(self.nc.sbuf_tensor(name, list(shape), dt))
        b = Buf(name, t)
        self.reg[name] = b
        return b

    def ps(self, name, shape, dt=F32):
        t = self.es.enter_context(self.nc.psum_tensor(name, list(shape), dt))
        b = Buf(name, t)
        self.reg[name] = b
        return b

    def bank(self):
        b = self.banks[self.bi % len(self.banks)]
        self.bi += 1
        return b

    def _dsem(self, buf):
        if buf.dkey is None:
            buf.dkey = "d_" + buf.name
            self.sems[buf.dkey] = self.es.enter_context(self.nc.semaphore(buf.dkey))
        return buf.dkey

    def _bufs(self, aps):
        out = []
        for a in aps:
            if a is None or isinstance(a, (int, float)):
                continue
            b = self.reg.get(a.name)
            if b is not None and b not in out:
                out.append(b)
        return out

    def _deps(self, eng, reads, writes, is_dma):
        w = {}

        def add(tok):
            if tok is None:
                return
            k, v = tok
            if k == eng and not is_dma and eng not in self.STRICT_ENGS:
                return
            if w.get(k, 0) < v:
                w[k] = v

        for b in reads:
            add(b.lw)
        for b in writes:
            add(b.lw)
            for t in b.rd:
                add(t)
        return w

    def op(self, eng, fn, rd, wr):
        reads = self._bufs(rd)
        writes = self._bufs(wr)
        w = self._deps(eng, reads, writes, False)
        self.cnt[eng] += 1
        tok = (eng, self.cnt[eng])
        for b in reads:
            if b not in writes:
                b.rd.append(tok)
        for b in writes:
            b.lw = tok
            b.rd = []
        self.ops[eng].append((w, fn, (eng, 1)))

    def dma(self, q, out_ap, in_ap, **kw):
        reads = self._bufs([in_ap])
        writes = self._bufs([out_ap])
        sembuf = (writes + reads)[0]
        w = self._deps(q, reads, writes, True)
        key = self._dsem(sembuf)
        sembuf.dcnt += 1
        tok = (key, 16 * sembuf.dcnt)
        for b in reads:
            b.rd.append(tok)
        for b in writes:
            b.lw = tok
            b.rd = []
        self.ops[q].append((w, lambda e: e.dma_start(out=out_ap, in_=in_ap, **kw), (key, 16)))
        return sembuf

    def final_wait(self, eng, bufs):
        w = {}
        for b in bufs:
            for tok in ([b.lw] + b.rd):
                if tok is None:
                    continue
                k, v = tok
                if w.get(k, 0) < v:
                    w[k] = v
        self.ops[eng].append((w, None, None))

    def mm(self, out, lhsT, rhs, start=True, stop=True):
        self.op("pe", lambda e: e.matmul(out, lhsT=lhsT, rhs=rhs, start=start, stop=stop), [lhsT, rhs], [out])

    def tr(self, out, in_, ident):
        self.op("pe", lambda e: e.transpose(out=out, in_=in_, identity=ident), [in_, ident], [out])

    def act(self, out, in_, func, bias=None, scale=None, accum=None):
        kw = {}
        if bias is not None:
            kw["bias"] = bias
        if scale is not None:
            kw["scale"] = scale
        if accum is not None:
            kw["accum_out"] = accum
        self.op("act", lambda e: e.activation(out=out, in_=in_, func=func, **kw), [in_, bias, scale], [out, accum])

    def tt(self, out, a, b, op, eng="dve"):
        self.op(eng, lambda e: e.tensor_tensor(out=out, in0=a, in1=b, op=op), [a, b], [out])

    def ts(self, out, a, s1, op0, s2=None, op1=None, eng="dve"):
        if op1 is None:
            self.op(eng, lambda e: e.tensor_scalar(out=out, in0=a, scalar1=s1, scalar2=None, op0=op0), [a, s1], [out])
        else:
            self.op(eng, lambda e: e.tensor_scalar(out=out, in0=a, scalar1=s1, scalar2=s2, op0=op0, op1=op1),
                    [a, s1, s2], [out])

    def stt(self, out, a, scalar, b, op0, op1, eng="dve"):
        self.op(eng, lambda e: e.scalar_tensor_tensor(out=out, in0=a, scalar=scalar, in1=b, op0=op0, op1=op1),
                [a, scalar, b], [out])

    def cp(self, out, in_, eng="dve"):
        if eng == "act":
            self.act(out, in_, AF.Copy)
        else:
            self.op(eng, lambda e: e.tensor_copy(out=out, in_=in_), [in_], [out])

    def ms(self, out, val, eng="dve"):
        self.op(eng, lambda e: e.memset(out, val), [], [out])

    def scan(self, out, d0, d1):
        self.op("dve", lambda e: e.tensor_tensor_scan(out=out, data0=d0, data1=d1, initial=0.0,
                                                      op0=ALU.mult, op1=ALU.add), [d0, d1], [out])

    def emit(self):
        nc = self.nc
        prog = self

        def run(engname, e):
            waited = {}
            for (w, fn, inc) in prog.ops[engname]:
                for k, v in w.items():
                    if waited.get(k, 0) < v:
                        e.wait_ge(prog.sems[k], v)
                        waited[k] = v
                if fn is not None:
                    fn(e).then_inc(prog.sems[inc[0]], inc[1])

        with nc.Block() as block:
            @block.tensor
            def _(e):
                run("pe", e)

            @block.scalar
            def _(e):
                run("act", e)

            @block.vector
            def _(e):
                run("dve", e)

            @block.gpsimd
            def _(e):
                run("pool", e)

            @block.sync
            def _(e):
                run("sp", e)
        self.es.close()


def bc(ap, shape, axis):
    return ap.unsqueeze(axis).to_broadcast(list(shape))


def build_nc(NPT=16, do_sample=True):
    nc = bass.Bass("TRN2", target_bir_lowering=False)

    def din(name, shape):
        return nc.dram_tensor(name, list(shape), F32, kind="ExternalInput").ap()

    def dout(name, shape):
        return nc.dram_tensor(name, list(shape), F32, kind="ExternalOutput").ap()

    xp = din("xp", [2048, 1024])
    xs = din("xs", [128, 1024])
    s_conv = din("s_conv", [16, 3, 3072])
    s_ssm = din("s_ssm", [16, 32, 64, 128])
    s_shift = din("s_shift", [16, 3328])
    s_rwkv = din("s_rwkv", [16, 16, 64, 64])
    s_ffn = din("s_ffn", [16, 2, 2816])
    W = {"in": din("w_in", [1024, IN_DIM]), "a": din("w_a", [2048, 1024]), "b": din("w_b", [1024, 1024]),
         "out": din("w_out", [1024, 1024]), "up": din("w_up", [1024, 2 * D_FF]), "down": din("w_down", [D_FF, 1024])}
    lora_wa = din("lora_wa", [128, 1024])
    lora_g = din("lora_g", [128, 1024])
    prow = din("prow", [NPR, 128])
    pbc_d = din("pbc_in", [NPB])
    cst_d = din("cst_in", [128, NCST])

    y_p = dout("y_p", [2048, 1024])
    y_s = dout("y_s", [128, 1024])
    o_conv_p = dout("o_conv_p", [3, 3072])
    o_ssm_p = dout("o_ssm_p", [32, 64, 128])
    o_shift_p = dout("o_shift_p", [1, 3328])
    o_rwkv_p = dout("o_rwkv_p", [16, 64, 64])
    o_ffn_p = dout("o_ffn_p", [2, 2816])
    o_conv_s = dout("o_conv_s", [16, 3, 3072])
    o_ssm_s = dout("o_ssm_s", [16, 32, 64, 128])
    o_shift_s = dout("o_shift_s", [16, 3328])
    o_rwkv_s = dout("o_rwkv_s", [16, 16, 64, 64])
    o_ffn_s = dout("o_ffn_s", [16, 2, 2816])

    p = Prog(nc)
    for i in range(8):
        p.banks.append(p.ps(f"pb{i}", [128, 512]))

    cst = p.sb("cst", [128, NCST])
    identb = p.sb("identb", [128, 128], BF16)
    UUb = p.sb("UUb", [128, 2, 128], BF16)
    BLKb = p.sb("BLKb", [128, 128], BF16)
    parF = p.sb("parF", [128, NPR])
    pbc = p.sb("pbc", [128, NPB])
    Ab = p.sb("Ab", [128, 32])
    wa_up = p.sb("wa_up", [128, 1024], BF16)
    g_up = p.sb("g_up", [128, 1024], BF16)
    NS = 5
    slots = [p.sb(f"ws{i}", [128, 8, 512], BF16) for i in range(NS)]
    xres = [p.sb(f"xres{i}", [128, 1024]) for i in range(2)]
    hT = p.sb("hT", [128, 8, 128], BF16)
    projT = p.sb("projT", [128, 26, 176])
    carry_conv = p.sb("carry_conv", [128, 24, 3])
    carry_shift = p.sb("carry_shift", [128, 26, 1])
    carry_ffn = p.sb("carry_ffn", [128, 22, 2])
    xcT = p.sb("xcT", [128, 24, 128], BF16)
    zs2 = p.sb("zs2", [128, 2048], BF16)
    uaT = p.sb("uaT", [128, 8, 128], BF16)
    ubT = p.sb("ubT", [128, 8, 128], BF16)
    ybT = p.sb("ybT", [128, 8, 128], BF16)
    ugblk = [p.sb(f"ugblk{i}", [128, 4, 160]) for i in range(2)]
    sm_r = p.sb("sm_r", [128, 4])
    sm_d = p.sb("sm_d", [128, 160])
    sm_q = p.sb("sm_q", [128, 8])
    sm_g = p.sb("sm_g", [128, 128])
    sm_p = p.sb("sm_p", [128, 256])
    BG = [p.sb(f"BG{i}", [128, 2048]) for i in range(4)]
    HB = [p.sb(f"HB{i}", [128, 2048], BF16) for i in range(4)]
    yaT = xcT[:, 0:16, :]
    mT = ybT
    actT = BG[1][:].bitcast(BF16)[:, 0:2816].rearrange("p (c t) -> p c t", t=128)
    hb = HB[2][:, 0:1024]
    rowsbuf = HB[3][:].bitcast(F32)
    G16 = p.sb("G16", [128, 8, 128], BF16)
    lT = p.sb("lT", [128, 2, 128])
    lb = p.sb("lb", [128, 3, 128], BF16)
    B_tok = p.sb("B_tok", [128, 512], BF16)
    MTs = [p.sb(f"MT{i}", [128, 8, 128], BF16) for i in range(2)]
    cbms = [p.sb(f"cbm{i}", [128, 128]) for i in range(2)]
    MT = MTs[0]
    hst = p.sb("hst", [128, 2048])
    hTbf = p.sb("hTbf", [128, 2048], BF16)
    Hst = p.sb("Hst", [128, 8, 64])
    Hbf = p.sb("Hbf", [128, 8, 64], BF16)
    NYb = [p.sb(f"NY{i}", [128, 4, 2, 128], BF16) for i in range(2)]
    AYb = [p.sb(f"AY{i}", [128, 4, 2, 128], BF16) for i in range(2)]
    Nq = [[p.sb(f"Nq{i}{k}", [128, 4, 128], BF16) for k in range(2)] for i in range(2)]
    Mq = [[p.sb(f"Mq{i}{k}", [128, 4, 128], BF16) for k in range(2)] for i in range(2)]
    Rb = [p.sb(f"Rb{i}", [128, 4, 128], BF16) for i in range(2)]
    W_sbs = [p.sb(f"W_sb{i}", [128, 4, 64], BF16) for i in range(2)]
    U_sbs = [p.sb(f"U_sb{i}", [128, 4, 64], BF16) for i in range(2)]
    W1Ts = [p.sb(f"W1T{i}", [128, 2, 128], BF16) for i in range(2)]

    ID = cst[:, C_ID:C_ID + 128]
    ONES = cst[:, C_ONES:C_ONES + 128]
    BLK = cst[:, C_BLK:C_BLK + 128]
    SEQ = cst[:, C_SEQ:C_SEQ + 16]

    def par(name, i=0, n=1):
        return parF[:, PR[name] + i:PR[name] + i + n]

    p.dma("sp", cst[:], cst_d)
    p.dma("sp", pbc[:], pbc_d.partition_broadcast(128))
    p.dma("pool", wa_up[:], lora_wa)
    p.dma("pool", g_up[:], lora_g)
    p.cp(identb[:], ID)
    p.cp(BLKb[:], cst[:, C_BLK:C_BLK + 128])
    p.cp(UUb[:, 0, :], cst[:, C_UUP:C_UUP + 128])
    p.cp(UUb[:, 1, :], cst[:, C_UUS:C_UUS + 128])
    p.act(Ab[:], pbc[:, PB_ALOG:PB_ALOG + 32], AF.Exp)
    p.ts(Ab[:], Ab[:], -1.0, ALU.mult)
    r0 = 0
    while r0 < NPR:
        nr = min(128, NPR - r0)
        st = BG[0]
        p.dma("sp", st[0:nr, 0:128], prow[r0:r0 + nr, :])
        pb = p.bank()
        p.mm(pb[:, 0:nr], lhsT=st[0:nr, 0:128], rhs=cst[0:nr, C_ID:C_ID + nr])
        p.cp(parF[:, r0:r0 + nr], pb[:, 0:nr])
        r0 += nr
    for nm, n in (("cw", 96), ("cb", 24), ("fw", 66), ("fb", 22), ("w0", 8), ("a0", 8)):
        p.ts(par(nm, 0, n), par(nm, 0, n), 0.5, ALU.mult)
    p.ms(hst[:], 0.0)
    p.ms(hTbf[:], 0.0)
    p.ms(Hst[:], 0.0)
    p.ms(Hbf[:], 0.0)
    p.ms(carry_conv[:], 0.0)
    p.ms(carry_shift[:], 0.0)
    p.ms(carry_ffn[:], 0.0)

    def tile_plan():
        pl = []
        for i in range(6):
            pl.append(("in", 0, 8, OFF_XBC + 512 * i, 512))
        pl.append(("in", 0, 8, OFF_DT, 32))
        for i in range(4):
            pl.append(("in", 0, 8, OFF_Z + 512 * i, 512))
        for i in range(6):
            pl.append(("in", 0, 8, OFF_RW + 512 * i, 512))
        pl.append(("in", 0, 8, OFF_RW + 3072, 256))
        for cb_ in range(2):
            for kb in range(2):
                pl.append(("a", 8 * kb, 8, 512 * cb_, 512))
        for cb_ in range(2):
            pl.append(("b", 0, 8, 512 * cb_, 512))
        for i in range(4):
            pl.append(("in", 0, 8, OFF_G + 512 * i, 512))
        for cb_ in range(2):
            pl.append(("out", 0, 8, 512 * cb_, 512))
        for i in range(6):
            n = 512 if i < 5 else 256
            pl.append(("up", 0, 8, 512 * i, n))
            pl.append(("up", 0, 8, D_FF + 512 * i, n))
        for cb_ in range(2):
            for kb, nr_ in ((0, 8), (1, 8), (2, 6)):
                pl.append(("down", 8 * kb, nr_, 512 * cb_, 512))
        return pl

    wbf = {}
    for key in ("in", "a", "b", "out", "up", "down"):
        rows, cols = W[key].shape
        t_ = nc.dram_tensor("wbf_" + key, [rows, cols], BF16, kind="Internal")
        p.reg["wbf_" + key] = Buf("wbf_" + key, t_)
        wbf[key] = t_.ap()

    ntiles = NPT + (1 if do_sample else 0)
    plan = tile_plan() * ntiles
    wstate = {"issued": 0, "cur": 0}

    REG = {"in": [(OFF_XBC, OFF_RW), (OFF_Z, OFF_XBC), (OFF_RW, OFF_G), (OFF_G, IN_DIM)],
           "a": [(0, 1024)], "b": [(0, 1024)], "out": [(0, 1024)], "up": [(0, D_FF), (D_FF, 2 * D_FF)],
           "down": [(0, 1024)]}
    cast_done = set()

    def ensure_cast(key, c0):
        for (lo, hi) in REG[key]:
            if lo <= c0 < hi:
                if (key, lo) not in cast_done:
                    cast_done.add((key, lo))
                    rows = W[key].shape[0]
                    step = 256 if (hi - lo) > 2048 else 512
                    for r_ in range(0, rows, step):
                        r1_ = min(rows, r_ + step)
                        p.dma("pool", wbf[key][r_:r1_, lo:hi], W[key][r_:r1_, lo:hi])
                return
        raise AssertionError((key, c0))

    def w_issue(upto):
        while wstate["issued"] < min(upto, len(plan)):
            i = wstate["issued"]
            key, r0_, nr_, c0, ncol = plan[i]
            ensure_cast(key, c0)
            s = slots[i % NS]
            src = wbf[key][r0_ * 128:(r0_ + nr_) * 128, c0:c0 + ncol].rearrange("(k p) n -> p k n", p=128)
            p.dma("pool", s[:, 0:nr_, 0:ncol], src)
            wstate["issued"] += 1

    def w_next(expect, issue=True):
        i = wstate["cur"]
        assert plan[i][0] == expect[0] and plan[i][3] == expect[1], (plan[i], expect)
        if issue:
            w_issue(i + NS)
        assert wstate["issued"] > i
        wstate["cur"] += 1
        return slots[i % NS]

    out_bufs = []

    def st(q, dram_ap, sb_ap, **kw):
        b = p.dma(q, dram_ap, sb_ap, **kw)
        if b not in out_bufs:
            out_bufs.append(b)

    def emit_rows(getc, nch, nrows, dram2d, stage):
        c = 0
        while c < nch:
            g0 = c
            while c < nch and c - g0 < 16:
                pb = p.bank()
                n4 = min(4, nch - c, 16 - (c - g0))
                for cc in range(n4):
                    p.mm(pb[0:nrows, cc * 128:(cc + 1) * 128], lhsT=getc(c + cc), rhs=ID)
                p.cp(stage[0:nrows, (c - g0) * 128:(c - g0 + n4) * 128], pb[0:nrows, 0:n4 * 128], eng="act")
                c += n4
            st("sp", dram2d[:, g0 * 128:c * 128], stage[0:nrows, 0:(c - g0) * 128])

    xloaded = {}
    def do_tile(ti, is_s):
        last_p = (not is_s) and ti == NPT - 1
        xr = xres[ti % 2]
        TRI = cst[:, (C_TRIS if is_s else C_TRIP):(C_TRIS if is_s else C_TRIP) + 128]
        SU = cst[:, (C_SUS if is_s else C_SUP):(C_SUS if is_s else C_SUP) + 128]
        UU = cst[:, (C_UUS if is_s else C_UUP):(C_UUS if is_s else C_UUP) + 128]
        MSK2 = cst[:, (C_SUS if is_s else C_SUP):(C_SUS if is_s else C_SUP) + 256]
        SCN = cst[:, C_RST:C_RST + 128] if is_s else ONES
        xsrc = xs if is_s else xp[ti * 128:(ti + 1) * 128, :]
        ydst = y_s if is_s else y_p[ti * 128:(ti + 1) * 128, :]

        def pv(c0, n, j, hist):
            if is_s:
                v = projT[:, c0:c0 + n, :].rearrange("p c (b t) -> p c b t", t=11)
                return v[:, :, :, 3 - hist + j:3 - hist + j + 8]
            return projT[:, c0:c0 + n, 3 - hist + j:3 - hist + j + 128]

        def tv(ap3):
            if is_s:
                return ap3.rearrange("p c (b t) -> p c b t", t=8)
            return ap3

        def bcp(ap2, n):
            if is_s:
                return ap2.unsqueeze(2).unsqueeze(3).to_broadcast([128, n, 16, 8])
            return ap2.unsqueeze(2).to_broadcast([128, n, 128])

        def rms_to_hT(src, gname):
            p.ms(sm_r[:, 0:1], 0.0)
            p.act(hb, src, AF.Square, accum=sm_r[:, 0:1])
            p.act(sm_r[:, 1:2], sm_r[:, 0:1], AF.Ln, bias=1e-5, scale=1.0 / 1024)
            p.act(sm_r[:, 1:2], sm_r[:, 1:2], AF.Exp, scale=-0.5)
            p.act(hb, src, AF.Copy, scale=sm_r[:, 1:2])
            pb = p.bank()
            pbv = pb[:].bitcast(BF16)
            for c in range(8):
                p.tr(pbv[:, c * 128:(c + 1) * 128], hb[:, c * 128:(c + 1) * 128], identb[:])
            p.tt(hT[:], pbv.rearrange("p (c t) -> p c t", t=128), bc(par(gname, 0, 8), [128, 8, 128], 2), ALU.mult)

        if not xloaded.get(ti):
            p.dma("sp", xr[:], xsrc)
        rms_to_hT(xr[:], "n1g")

        if is_s:
            sc2 = s_conv.rearrange("b t c -> (b t) c")
            for piece in range(3):
                p.dma("sp", rowsbuf[0:48, :], sc2[:, piece * 1024:(piece + 1) * 1024])
                pb = p.bank()
                for cc in range(8):
                    p.tr(pb[:, cc * 48:(cc + 1) * 48], rowsbuf[0:48, cc * 128:(cc + 1) * 128], cst[0:48, C_ID:C_ID + 48])
                dstv = projT[:, piece * 8:piece * 8 + 8, :].rearrange("p c (b t) -> p c b t", t=11)[:, :, :, 0:3]
                p.cp(dstv, pb[:, 0:384].rearrange("p (c b t) -> p c b t", c=8, b=16), eng="act")
        else:
            p.cp(projT[:, 0:24, 0:3], carry_conv[:])

        for i in range(6):
            s = w_next(("in", OFF_XBC + 512 * i))
            pb = p.bank()
            for cc in range(4):
                for kc in range(8):
                    p.mm(pb[:, cc * 128:(cc + 1) * 128], lhsT=s[:, kc, cc * 128:(cc + 1) * 128], rhs=hT[:, kc, :],
                         start=(kc == 0), stop=(kc == 7))
            p.cp(pv(4 * i, 4, 3, 3), tv(pb[:].rearrange("p (c t) -> p c t", t=128)), eng="act")
        if is_s:
            cmpb = BG[2]
            cv = cmpb[:, 0:24 * 48].rearrange("p (c b t) -> p c b t", c=24, b=16)
            p.cp(cv, projT[:, 0:24, :].rearrange("p c (b t) -> p c b t", t=11)[:, :, :, 8:11])
            emit_rows(lambda c: cmpb[:, c * 48:(c + 1) * 48], 24, 48, o_conv_s.rearrange("b t c -> (b t) c"), BG[3])
        else:
            p.cp(carry_conv[:], projT[:, 0:24, 128:131])
            if last_p:
                emit_rows(lambda c: carry_conv[:, c, :], 24, 3, o_conv_p, BG[3])

        nxt = ti + 1
        if nxt < ntiles:
            nsrc = xs if nxt == NPT else xp[nxt * 128:(nxt + 1) * 128, :]
            p.dma("sp", xres[nxt % 2][:], nsrc)
            xloaded[nxt] = True
        s = w_next(("in", OFF_DT))
        pb = p.bank()
        for kc in range(8):
            p.mm(pb[:, 0:32], lhsT=hT[:, kc, :], rhs=s[:, kc, 0:32], start=(kc == 0), stop=(kc == 7))
        dtv = sm_d[:, 0:32]
        dtA = sm_d[:, 32:64]
        ex = sm_d[:, 64:160]
        p.tt(dtv, pb[:, 0:32], pbc[:, PB_DTB:PB_DTB + 32], ALU.add)
        p.act(dtv, dtv, AF.Exp)
        p.act(dtv, dtv, AF.Ln, bias=1.0)
        p.tt(dtA, dtv, Ab[:], ALU.mult)
        pbk = p.bank()
        p.mm(pbk[:, 0:32], lhsT=TRI, rhs=dtA)
        p.mm(pbk[:, 32:64], lhsT=UU, rhs=dtA)
        p.mm(pbk[:, 64:96], lhsT=ONES, rhs=dtA)
        p.act(ex, pbk[:, 0:96], AF.Exp)
        expcs = sm_d[:, 64:96]
        toend = sm_d[:, 96:128]
        decb = sm_d[:, 128:160]

        def accv(c):
            return BG[0][:, c * 128:(c + 1) * 128] if c < 16 else BG[1][:, (c - 16) * 128:(c - 15) * 128]

        def thv(c0):
            return BG[2][:, c0 * 128:(c0 + 8) * 128] if c0 < 16 else BG[1][:, 1024:2048]

        for c in range(24):
            p.act(tv(accv(c).unsqueeze(1)), pv(c, 1, 0, 3), AF.Identity, scale=par("cw", c), bias=par("cb", c))
        for c in range(24):
            a_ = tv(accv(c).unsqueeze(1))
            for j in range(1, 4):
                p.stt(a_, pv(c, 1, j, 3), par("cw", j * 24 + c), a_, ALU.mult, ALU.add)
        for c0 in (0, 8, 16):
            a8 = BG[0][:, c0 * 128:(c0 + 8) * 128] if c0 < 16 else BG[1][:, 0:1024]
            p.act(thv(c0), a8, AF.Tanh)
            p.stt(xcT[:, c0:c0 + 8, :].rearrange("p c t -> p (c t)"), thv(c0), 1.0, a8, ALU.add, ALU.mult)

        for i in range(4):
            s = w_next(("in", OFF_Z + 512 * i))
            pb = p.bank()
            for kc in range(8):
                p.mm(pb[:], lhsT=hT[:, kc, :], rhs=s[:, kc, :], start=(kc == 0), stop=(kc == 7))
            th = BG[3][:, 0:512]
            p.act(th, pb[:], AF.Tanh, scale=0.5)
            p.stt(zs2[:, i * 512:(i + 1) * 512], th, 1.0, pb[:], ALU.add, ALU.mult)

        def rw_hist():
            if is_s:
                for piece in range(4):
                    ncol = 1024 if piece < 3 else 256
                    p.dma("sp", rowsbuf[0:16, 0:ncol], s_shift[:, piece * 1024:piece * 1024 + ncol])
                    nchk = ncol // 128
                    pb = p.bank()
                    for cc in range(nchk):
                        p.tr(pb[:, cc * 16:(cc + 1) * 16], rowsbuf[0:16, cc * 128:(cc + 1) * 128], cst[0:16, C_ID:C_ID + 16])
                    dstv = projT[:, piece * 8:piece * 8 + nchk, :].rearrange("p c (b t) -> p c b t", t=11)[:, :, :, 2:3]
                    p.cp(dstv, pb[:, 0:nchk * 16].rearrange("p (c b t) -> p c b t", c=nchk, b=16), eng="act")
            else:
                p.cp(projT[:, 0:26, 2:3], carry_shift[:])

        def rw_block(i):
            n4 = 4 if i < 6 else 2
            s = w_next(("in", OFF_RW + 512 * i))
            pb = p.bank()
            for cc in range(n4):
                for kc in range(8):
                    p.mm(pb[:, cc * 128:(cc + 1) * 128], lhsT=s[:, kc, cc * 128:(cc + 1) * 128], rhs=hT[:, kc, :],
                         start=(kc == 0), stop=(kc == 7))
            p.cp(pv(4 * i, n4, 1, 1), tv(pb[:, 0:n4 * 128].rearrange("p (c t) -> p c t", t=128)), eng="act")

        def rw_post():
            if is_s:
                cmpb = BG[2]
                cv = cmpb[:, 0:26 * 16].rearrange("p (c b t) -> p c b t", c=26, b=16)
                p.cp(cv, projT[:, 0:26, :].rearrange("p c (b t) -> p c b t", t=11)[:, :, :, 10:11])
                emit_rows(lambda c: cmpb[:, c * 16:(c + 1) * 16], 26, 16, o_shift_s, BG[3])
            else:
                p.cp(carry_shift[:], projT[:, 0:26, 130:131])
                if last_p:
                    emit_rows(lambda c: carry_shift[:, c, :], 26, 1, o_shift_p, BG[3])

        rw_state = {"n": 0}

        def rw_some(k):
            if rw_state["n"] == 0:
                rw_hist()
            for _ in range(k):
                if rw_state["n"] < 7:
                    rw_block(rw_state["n"])
                    rw_state["n"] += 1
                    if rw_state["n"] == 7:
                        rw_post()

        x_tok, xdt, xD, xdts = HB[0], HB[1], HB[2], HB[3]
        for half in range(2):
            pb = p.bank()
            pbv = pb[:].bitcast(BF16)
            for j in range(8):
                p.tr(pbv[:, j * 128:(j + 1) * 128], xcT[:, half * 8 + j, :], identb[:])
            p.cp(x_tok[:, half * 1024:(half + 1) * 1024], pbv, eng="act")
        pb = p.bank()
        pbv = pb[:].bitcast(BF16)
        for g in range(4):
            p.tr(pbv[:, g * 128:(g + 1) * 128], xcT[:, 16 + g, :], identb[:])
        p.cp(B_tok[:], pbv[:, 0:512], eng="act")
        x3 = x_tok[:].rearrange("p (h d) -> p h d", d=64)
        p.tt(xdt[:].rearrange("p (h d) -> p h d", d=64), x3, bc(dtv, [128, 32, 64], 2), ALU.mult)
        p.tt(xD[:].rearrange("p (h d) -> p h d", d=64), x3, bc(pbc[:, PB_D:PB_D + 32], [128, 32, 64], 2), ALU.mult)
        p.tt(xdts[:].rearrange("p (h d) -> p h d", d=64), xdt[:].rearrange("p (h d) -> p h d", d=64),
             bc(toend, [128, 32, 64], 2), ALU.mult)

        yoff = BG[3]
        if is_s:
            dtx = BG[1]
            p.cp(dtx[:].rearrange("p (h d) -> p h d", d=64), bc(dtA, [128, 32, 64], 2))
            pbd = p.bank()
            for jj in range(16):
                p.mm(pbd[:, jj * 16:(jj + 1) * 16], lhsT=dtx[:, jj * 128:(jj + 1) * 128], rhs=SEQ)
            decP = sm_p[:, 0:256]
            p.act(decP, pbd[:, 0:256], AF.Exp)
            decP3 = decP.rearrange("p (j b) -> p j b", b=16)
            p.ms(yoff[:], 0.0)
            h0Ts = [actT[:].rearrange("p c t -> p (c t)")[:, 0:2048], yaT.rearrange("p c t -> p (c t)")]
            Bms = [MTs[i][:].rearrange("p h l -> p (h l)")[:, 0:512] for i in range(2)]

            def seqgen(b):
                stin = BG[0] if b % 2 == 0 else BG[2]
                h0T = h0Ts[b % 2]
                Bm = Bms[b % 2]
                src = s_ssm[b].rearrange("(j q) d n -> (q d) j n", q=2)
                p.dma("sp", stin[:].rearrange("p (j n) -> p j n", n=128), src)
                for g4 in range(4):
                    pb = p.bank()
                    for jj in range(4):
                        j = g4 * 4 + jj
                        p.tr(pb[:, jj * 128:(jj + 1) * 128], stin[:, j * 128:(j + 1) * 128], ID)
                    p.cp(h0T[:, g4 * 512:(g4 + 1) * 512], pb[:], eng="act")
                yield
                for gq in range(4):
                    pb = p.bank()
                    p.mm(pb[:], lhsT=xcT[:, 20 + gq, :], rhs=h0T[:, gq * 512:(gq + 1) * 512])
                    p.stt(yoff[:, gq * 512:(gq + 1) * 512], pb[:], SEQ[:, b:b + 1], yoff[:, gq * 512:(gq + 1) * 512],
                          ALU.mult, ALU.add)
                yield
                p.ts(Bm, B_tok[:], SEQ[:, b:b + 1], ALU.mult)
                p.tt(stin[:].rearrange("p (j n) -> p j n", n=128), stin[:].rearrange("p (j n) -> p j n", n=128),
                     bc(decP3[:, :, b], [128, 16, 128], 2), ALU.mult)
                for g4 in range(4):
                    pb = p.bank()
                    for jj in range(4):
                        j = g4 * 4 + jj
                        p.mm(pb[:, jj * 128:(jj + 1) * 128], lhsT=xdts[:, j * 128:(j + 1) * 128],
                             rhs=Bm[:, (j // 4) * 128:(j // 4 + 1) * 128])
                    p.tt(stin[:, g4 * 512:(g4 + 1) * 512], stin[:, g4 * 512:(g4 + 1) * 512], pb[:], ALU.add)
                st("sp", o_ssm_s[b].rearrange("(j q) d n -> (q d) j n", q=2), stin[:].rearrange("p (j n) -> p j n", n=128))

            for b0_ in range(0, 16, 2):
                live_ = [seqgen(b0_), seqgen(b0_ + 1)]
                while live_:
                    for g_ in list(live_):
                        try:
                            next(g_)
                        except StopIteration:
                            live_.remove(g_)

        ya_all = BG[1]
        ssqg = sm_q[:, 0:4]
        p.ms(ssqg, 0.0)
        def grp(gq):
            par_ = gq % 2
            rb_ = BG[0] if (par_ == 0 or is_s) else BG[3]
            Rbuf = rb_[:].bitcast(BF16)[:, 0:1024]
            Mexp = rb_[:, 1024:2048]
            ytmp = BG[2][:, par_ * 512:(par_ + 1) * 512]
            sqj = BG[2][:, 1024 + par_ * 512:1024 + (par_ + 1) * 512]
            MT = MTs[par_]
            cbm = cbms[par_]
            p.tt(Rbuf.rearrange("p (h l) -> p h l", l=128), bc(dtA[:, 8 * gq:8 * gq + 8], [128, 8, 128], 2),
                 bc(TRI, [128, 8, 128], 1), ALU.mult)
            for hf in range(2):
                pS = p.bank()
                p.mm(pS[:], lhsT=UUb[:, 1 if is_s else 0, :], rhs=Rbuf[:, hf * 512:(hf + 1) * 512])
                p.act(Mexp[:, hf * 512:(hf + 1) * 512], pS[:], AF.Exp)
            pcb = p.bank()
            p.mm(pcb[:, 0:128], lhsT=xcT[:, 16 + gq, :], rhs=xcT[:, 20 + gq, :])
            yield
            p.tt(cbm[:], pcb[:, 0:128], TRI, ALU.mult)
            p.tt(MT[:], Mexp.rearrange("p (h l) -> p h l", l=128), bc(cbm[:], [128, 8, 128], 1), ALU.mult)
            yield
            pY = p.bank()
            for h in range(8):
                col = (8 * gq + h) * 64
                p.mm(pY[:, h * 64:(h + 1) * 64], lhsT=MT[:, h, :], rhs=xdt[:, col:col + 64], start=True, stop=False)
                p.mm(pY[:, h * 64:(h + 1) * 64], lhsT=identb[:], rhs=xD[:, col:col + 64], start=False, stop=True)
            if is_s:
                p.tt(ytmp.rearrange("p (h d) -> p h d", d=64),
                     yoff[:, gq * 512:(gq + 1) * 512].rearrange("p (h d) -> p h d", d=64),
                     bc(expcs[:, 8 * gq:8 * gq + 8], [128, 8, 64], 2), ALU.mult)
            else:
                pO = p.bank()
                p.mm(pO[:], lhsT=xcT[:, 20 + gq, :], rhs=hTbf[:, gq * 512:(gq + 1) * 512])
                p.tt(ytmp.rearrange("p (h d) -> p h d", d=64), pO[:].rearrange("p (h d) -> p h d", d=64),
                     bc(expcs[:, 8 * gq:8 * gq + 8], [128, 8, 64], 2), ALU.mult)
            if not is_s:
                rw_some(1)
            yield
            p.tt(ytmp, ytmp, pY[:], ALU.add)
            yag = ya_all[:, gq * 512:(gq + 1) * 512]
            p.stt(yag, ytmp, 0.5, zs2[:, gq * 512:(gq + 1) * 512], ALU.mult, ALU.mult)
            p.act(sqj, yag, AF.Square, accum=ssqg[:, gq:gq + 1])

        for g0_ in (0, 2):
            gens_ = [grp(g0_), grp(g0_ + 1)]
            if is_s:
                for g_ in gens_:
                    for _ in g_:
                        pass
            else:
                live_ = list(gens_)
                while live_:
                    for g_ in list(live_):
                        try:
                            next(g_)
                        except StopIteration:
                            live_.remove(g_)
        rstg = sm_q[:, 4:8]
        p.act(rstg, ssqg, AF.Ln, bias=1e-5, scale=1.0 / 512)
        p.act(rstg, rstg, AF.Exp, scale=-0.5)
        yn = HB[0]
        for gq in range(4):
            p.ts(yn[:, gq * 512:(gq + 1) * 512], ya_all[:, gq * 512:(gq + 1) * 512], rstg[:, gq:gq + 1], ALU.mult)
        for half in range(2):
            pb = p.bank()
            pbv = pb[:].bitcast(BF16)
            for j in range(8):
                p.tr(pbv[:, j * 128:(j + 1) * 128], yn[:, (half * 8 + j) * 128:(half * 8 + j + 1) * 128], identb[:])
            p.tt(yaT[:, half * 8:half * 8 + 8, :], pbv.rearrange("p (c t) -> p c t", t=128),
                 bc(par("ng", half * 8, 8), [128, 8, 128], 2), ALU.mult)
        if not is_s:
            for gq in range(4):
                pst = p.bank()
                p.mm(pst[:], lhsT=B_tok[:, gq * 128:(gq + 1) * 128], rhs=xdts[:, gq * 512:(gq + 1) * 512])
                hv = hst[:, gq * 512:(gq + 1) * 512]
                p.tt(hv.rearrange("p (h d) -> p h d", d=64), hv.rearrange("p (h d) -> p h d", d=64),
                     bc(decb[:, 8 * gq:8 * gq + 8], [128, 8, 64], 2), ALU.mult)
                p.tt(hv, hv, pst[:], ALU.add)
                p.cp(hTbf[:, gq * 512:(gq + 1) * 512], hv, eng="act")
            if last_p:
                stg = BG[0]
                for g4 in range(4):
                    pb = p.bank()
                    for jj in range(4):
                        j = g4 * 4 + jj
                        p.tr(pb[:, jj * 128:(jj + 1) * 128], hst[:, j * 128:(j + 1) * 128], ID)
                    p.cp(stg[:, g4 * 512:(g4 + 1) * 512], pb[:], eng="act")
                st("sp", o_ssm_p.rearrange("(j q) d n -> (q d) j n", q=2), stg[:].rearrange("p (j n) -> p j n", n=128))

        def do_ua():
            for cb_ in range(2):
                pb = p.bank()
                sw = [w_next(("a", 512 * cb_)), w_next(("a", 512 * cb_), issue=False)]
                for cc in range(4):
                    for kb in range(2):
                        for kc in range(8):
                            p.mm(pb[:, cc * 128:(cc + 1) * 128], lhsT=sw[kb][:, kc, cc * 128:(cc + 1) * 128],
                                 rhs=yaT[:, kb * 8 + kc, :], start=(kb == 0 and kc == 0), stop=(kb == 1 and kc == 7))
                p.cp(uaT[:, cb_ * 4:cb_ * 4 + 4, :], pb[:].rearrange("p (c t) -> p c t", t=128), eng="act")

        rw_some(7)

        def F(i):
            return BG[i // 2][:, (i % 2) * 1024:(i % 2 + 1) * 1024].rearrange("p (c t) -> p c t", t=128)

        rT, kT, vT, lw, aT, kkb, f6, f7 = [F(i) for i in range(8)]

        def mix(dst, c0, n):
            tmp = f7[:, 0:n, :]
            p.tt(tv(tmp), pv(c0, n, 0, 1), pv(c0, n, 1, 1), ALU.subtract)
            p.tt(tv(tmp), tv(tmp), bcp(par("mu", c0, n), n), ALU.mult)
            p.tt(tv(dst), tv(tmp), pv(c0, n, 1, 1), ALU.add)

        mix(rT, 0, 8)
        mix(kT, 8, 8)
        mix(vT, 16, 8)
        mix(lT[:], 24, 2)
        p.tt(kkb, kT, bc(par("kk", 0, 8), [128, 8, 128], 2), ALU.mult)
        sqb = HB[0][:, 0:1024]
        rkb = HB[0][:, 1024:2048]
        p.tt(sqb.rearrange("p (c t) -> p c t", t=128), kkb, kkb, ALU.mult)
        for hf in range(2):
            pq = p.bank()
            p.mm(pq[:], lhsT=BLKb[:], rhs=sqb[:, hf * 512:(hf + 1) * 512])
            p.ts(f6[:, hf * 4:hf * 4 + 4, :], pq[:].rearrange("p (c t) -> p c t", t=128), 1e-24, ALU.max)
        p.act(f6, f6, AF.Ln)
        p.act(f6, f6, AF.Exp, scale=-0.5)
        p.tt(kkb, kkb, f6, ALU.mult)
        p.ms(lb[:, 0, :], 0.0)
        p.ms(lb[:, 2, :], 0.0)
        p.act(lb[0:64, 0, :], lT[0:64, 0, :], AF.Tanh)
        p.cp(lb[64:128, 2, :], lT[64:128, 0, :])
        p.act(lT[:, 1, :], lT[:, 1, :], AF.Tanh, scale=0.5)
        p.ts(lb[:, 1, :], lT[:, 1, :], 0.5, ALU.mult, 0.5, ALU.add)
        for c in range(8):
            pL = p.bank()
            p.mm(pL[:, 0:128], lhsT=wa_up[:, c * 128:(c + 1) * 128], rhs=lb[:, 0, :])
            p.mm(pL[:, 128:256], lhsT=wa_up[:, c * 128:(c + 1) * 128], rhs=lb[:, 2, :])
            p.mm(pL[:, 256:384], lhsT=g_up[:, c * 128:(c + 1) * 128], rhs=lb[:, 1, :])
            p.act(lw[:, c, :], pL[:, 0:128], AF.Tanh, scale=0.5, bias=par("w0", c))
            p.act(aT[:, c, :], pL[:, 128:256], AF.Tanh, scale=0.5, bias=par("a0", c))
            p.cp(G16[:, c, :], pL[:, 256:384], eng="act")
        do_ua()
        CW = -0.5 * float(np.exp(-0.5))
        p.ts(lw, lw, CW, ALU.mult, CW, ALU.add)
        p.ts(aT, aT, 0.5, ALU.mult, 0.5, ALU.add)
        p.stt(f6, aT, -1.0, bc(par("ka", 0, 8), [128, 8, 128], 2), ALU.add, ALU.mult)
        p.stt(kT, f6, 1.0, kT, ALU.add, ALU.mult)
        p.tt(f6, rT, kT, ALU.mult)
        p.tt(rkb.rearrange("p (c t) -> p c t", t=128), f6, bc(par("rk", 0, 8), [128, 8, 128], 2), ALU.mult)
        vbf = HB[2][:, 0:1024].rearrange("p (c t) -> p c t", t=128)
        p.cp(vbf, vT)
        for hf in range(2):
            pq = p.bank()
            p.mm(pq[:], lhsT=BLKb[:], rhs=rkb[:, hf * 512:(hf + 1) * 512])
            p.tt(f6[:, hf * 4:hf * 4 + 4, :], pq[:].rearrange("p (c t) -> p c t", t=128), vT[:, hf * 4:hf * 4 + 4, :], ALU.mult)
        bv = f6
        p.tt(aT, kkb, aT, ALU.mult)
        csT = vT
        for c in range(8):
            p.scan(csT[:, c, :], SCN, lw[:, c, :])
        AR = HB[0][:].rearrange("p (c a t) -> p c a t", a=2, t=128)
        bT = HB[1][:, 0:1024].rearrange("p (c t) -> p c t", t=128)
        kT2 = HB[1][:, 1024:2048].rearrange("p (c t) -> p c t", t=128)
        p.tt(lw, csT, lw, ALU.subtract)
        p.act(lw, lw, AF.Exp)
        p.stt(AR[:, :, 0, :], kkb, -1.0, lw, ALU.mult, ALU.mult)
        p.act(lw, csT, AF.Exp)
        p.tt(AR[:, :, 1, :], rT, lw, ALU.mult)
        glast = sm_g[:, 0:128]
        if is_s:
            p.cp(glast.rearrange("p (c b) -> p c b", b=16), lw.rearrange("p c (b t) -> p c b t", t=8)[:, :, :, 7])
        else:
            p.cp(glast[:, 0:8], lw[:, :, 127])
        p.act(lw, csT, AF.Exp, scale=-1.0)
        p.tt(bT, aT, lw, ALU.mult)
        p.tt(kT2, kT, lw, ALU.mult)
        V_tok = HB[2][:, 1024:2048]
        b_tok = HB[3][:, 0:1024]
        k_tok = HB[3][:, 1024:2048]
        for src3, dst in ((vbf, V_tok), (bT, b_tok), (kT2, k_tok)):
            pb = p.bank()
            pbv = pb[:].bitcast(BF16)
            for c in range(8):
                p.tr(pbv[:, c * 128:(c + 1) * 128], src3[:, c, :], identb[:])
            p.cp(dst, pbv, eng="act")

        ysb = F(0)

        def heads(bi):
            return [(2 * bi + jj, q) for q in range(2) for jj in range(2)]

        def abc(bi):
            k = bi % 2
            j0 = 2 * bi
            NY, AY = NYb[k], AYb[k]
            pa = [p.bank(), p.bank()]
            pbk_ = [p.bank(), p.bank()]
            pc = [p.bank(), p.bank()]
            for hh, (j, q) in enumerate(heads(bi)):
                jj = j - j0
                qs = slice(64 * q, 64 * q + 64)
                arr = AR[qs, j, :, :].rearrange("p a t -> p (a t)")
                p.mm(pa[q][:, jj * 256:(jj + 1) * 256], lhsT=bT[qs, j, :], rhs=arr)
                p.mm(pbk_[q][:, jj * 256:(jj + 1) * 256], lhsT=kT2[qs, j, :], rhs=arr)
                p.mm(pc[q][:, jj * 128:(jj + 1) * 128], lhsT=AR[qs, j, 0, :], rhs=bT[qs, j, :])
            m2 = bc(MSK2.rearrange("p (a t) -> p a t", a=2), [128, 2, 2, 128], 1)
            for q in range(2):
                p.tt(NY[:, 2 * q:2 * q + 2, :, :], pa[q][:].rearrange("p (h a t) -> p h a t", h=2, a=2), m2, ALU.mult)
                p.tt(AY[:, 2 * q:2 * q + 2, :, :], pbk_[q][:].rearrange("p (h a t) -> p h a t", h=2, a=2), m2, ALU.mult)
                p.tt(Mq[k][0][:, 2 * q:2 * q + 2, :], pc[q][:, 0:256].rearrange("p (h t) -> p h t", t=128),
                     bc(UU, [128, 2, 128], 1), ALU.mult)
            p.cp(Nq[k][0][:], NY[:, :, 0, :])
            p.tt(Rb[k][:], NY[:, :, 0, :], bc(ID, [128, 4, 128], 1), ALU.add)

        def inv_step(bi, i):
            k = bi % 2
            Nc, Mc = Nq[k][i % 2], Mq[k][i % 2]
            Nn, Mn = Nq[k][(i + 1) % 2], Mq[k][(i + 1) % 2]
            if i >= 1:
                p1 = p.bank()
                for hh in range(4):
                    p.mm(p1[:, hh * 128:(hh + 1) * 128], lhsT=Mc[:, hh, :], rhs=Rb[k][:, hh, :])
            if i <= 4:
                p2 = p.bank()
                for hh in range(4):
                    p.mm(p2[:, hh * 128:(hh + 1) * 128], lhsT=Mc[:, hh, :], rhs=Nc[:, hh, :])
            if i <= 5:
                p3 = p.bank()
                for hh in range(4):
                    p.mm(p3[:, hh * 128:(hh + 1) * 128], lhsT=Nc[:, hh, :], rhs=Mc[:, hh, :])
            if i >= 1:
                p.tt(Rb[k][:], Rb[k][:], p1[:].rearrange("p (h t) -> p h t", t=128), ALU.add)
            if i <= 4:
                p.cp(Nn[:], p2[:].rearrange("p (h t) -> p h t", t=128), eng="act")
            if i <= 5:
                p.cp(Mn[:], p3[:].rearrange("p (h t) -> p h t", t=128), eng="act")

        def finish(bi):
            k = bi % 2
            W_sb, U_sb = W_sbs[k], U_sbs[k]
            NY, AY = NYb[k], AYb[k]
            j0 = 2 * bi
            hs = heads(bi)
            if is_s:
                if k == 0:
                    Sin, Hall, Hallb = BG[2][:], BG[1][:], zs2[:]
                    UV = yaT.rearrange("p c t -> p (c t)")
                else:
                    pj = projT[:].rearrange("p c t -> p (c t)")
                    Sin, Hall, Hallb = pj[:, 0:2048], pj[:, 2048:4096], hst[:].bitcast(BF16)[:, 0:2048]
                    UV = hTbf[:]
                W1T = W1Ts[k]
                S4 = Sin.rearrange("p (b j k) -> p b j k", b=16, j=2)
                H4 = Hall.rearrange("p (b j v) -> p b j v", b=16, j=2)
                Hb4 = Hallb.rearrange("p (b j v) -> p b j v", b=16, j=2)
                for jj_ in range(2):
                    p.dma("sp", S4[:, :, jj_, :], s_rwkv.rearrange("b (j q) v k -> (q v) b j k", q=2)[:, :, j0 + jj_, :])
                for g in range(4):
                    pbq = [p.bank(), p.bank()]
                    for e8 in range(8):
                        b, jj = divmod(g * 8 + e8, 2)
                        for q in range(2):
                            qs = slice(64 * q, 64 * q + 64)
                            p.mm(pbq[q][qs, e8 * 64:(e8 + 1) * 64], lhsT=S4[qs, b, jj, :],
                                 rhs=cst[qs, C_ID + 64 * q:C_ID + 64 * q + 64])
                    for q in range(2):
                        qs = slice(64 * q, 64 * q + 64)
                        p.cp(Hall[qs, g * 512:(g + 1) * 512], pbq[q][qs, :], eng="act")
                p.cp(Hallb, Hall)
                yield
                pw1 = [p.bank(), p.bank()]
                for hh, (j, q) in enumerate(hs):
                    qs = slice(64 * q, 64 * q + 64)
                    jj = j - j0
                    for b in range(16):
                        p.mm(pw1[q][qs, jj * 128 + 8 * b:jj * 128 + 8 * b + 8], lhsT=Hb4[qs, b, jj, :],
                             rhs=AR[qs, j, 0, 8 * b:8 * b + 8])
                for q in range(2):
                    qs = slice(64 * q, 64 * q + 64)
                    p.cp(W1T[qs, :, :], pw1[q][qs, 0:256].rearrange("p (j t) -> p j t", t=128), eng="act")
            pW = [p.bank(), p.bank()]
            for hh, (j, q) in enumerate(hs):
                qs = slice(64 * q, 64 * q + 64)
                hc = slice((2 * j + q) * 64, (2 * j + q) * 64 + 64)
                jj = j - j0
                o = pW[q][:, jj * 64:(jj + 1) * 64]
                if is_s:
                    p.mm(o, lhsT=W1T[qs, jj, :], rhs=identb[qs, 64 * q:64 * q + 64], start=True, stop=False)
                else:
                    p.mm(o, lhsT=AR[qs, j, 0, :], rhs=Hbf[qs, j, :], start=True, stop=False)
                p.mm(o, lhsT=AY[:, hh, 0, :], rhs=V_tok[:, hc], start=False, stop=True)
            for q in range(2):
                p.cp(W_sb[:, 2 * q:2 * q + 2, :], pW[q][:, 0:128].rearrange("p (h v) -> p h v", v=64), eng="act")
            yield
            pU = p.bank()
            for hh in range(4):
                p.mm(pU[:, hh * 64:(hh + 1) * 64], lhsT=Rb[k][:, hh, :], rhs=W_sb[:, hh, :])
            p.cp(U_sb[:], pU[:, 0:256].rearrange("p (h v) -> p h v", v=64), eng="act")
            yield
            pYo = [p.bank(), p.bank()]
            for hh, (j, q) in enumerate(hs):
                qs = slice(64 * q, 64 * q + 64)
                hc = slice((2 * j + q) * 64, (2 * j + q) * 64 + 64)
                jj = j - j0
                o = pYo[q][qs, jj * 128:(jj + 1) * 128]
                p.mm(o, lhsT=U_sb[:, hh, :], rhs=NY[:, hh, 1, :], start=True, stop=False)
                p.mm(o, lhsT=V_tok[:, hc], rhs=AY[:, hh, 1, :], start=False, stop=False)
                if is_s:
                    for b in range(16):
                        p.mm(pYo[q][qs, jj * 128 + 8 * b:jj * 128 + 8 * b + 8], lhsT=Hb4[qs, b, jj, :],
                             rhs=AR[qs, j, 1, 8 * b:8 * b + 8], start=False, stop=(b == 15))
                else:
                    p.mm(o, lhsT=Hbf[qs, j, :], rhs=AR[qs, j, 1, :], start=False, stop=True)
            for q in range(2):
                qs = slice(64 * q, 64 * q + 64)
                p.cp(ysb[qs, j0:j0 + 2, :], pYo[q][qs, 0:256].rearrange("p (j t) -> p j t", t=128), eng="act")
            if is_s:
                yield
                Ublk = UV[:, 0:1024].rearrange("p (b v) -> p b v", v=64)
                Vblk = UV[:, 1024:2048].rearrange("p (b v) -> p b v", v=64)
                sq3 = bc(SEQ, [128, 16, 64], 2)
                for jj in range(2):
                    pd = [p.bank(), p.bank()]
                    for q in range(2):
                        hh = 2 * q + jj
                        j = j0 + jj
                        qs = slice(64 * q, 64 * q + 64)
                        hc = slice((2 * j + q) * 64, (2 * j + q) * 64 + 64)
                        p.tt(Ublk, bc(U_sb[:, hh, :], [128, 16, 64], 1), sq3, ALU.mult)
                        p.tt(Vblk, bc(V_tok[:, hc], [128, 16, 64], 1), sq3, ALU.mult)
                        for hf in range(2):
                            p.mm(pd[hf][qs, :], lhsT=b_tok[:, hc], rhs=Ublk[:, hf * 8:hf * 8 + 8, :].rearrange("p b v -> p (b v)"),
                                 start=True, stop=False)
                            p.mm(pd[hf][qs, :], lhsT=k_tok[:, hc], rhs=Vblk[:, hf * 8:hf * 8 + 8, :].rearrange("p b v -> p (b v)"),
                                 start=False, stop=True)
                    for hf in range(2):
                        hv = H4[:, hf * 8:hf * 8 + 8, jj, :]
                        p.tt(hv, hv, pd[hf][:].rearrange("p (b v) -> p b v", v=64), ALU.add)
                    g3 = glast.rearrange("p (c b) -> p c b", b=16)[:, j0 + jj, :]
                    p.tt(H4[:, :, jj, :], H4[:, :, jj, :], bc(g3, [128, 16, 64], 2), ALU.mult)
                for g in range(4):
                    pbq = [p.bank(), p.bank()]
                    for e8 in range(8):
                        b, jj = divmod(g * 8 + e8, 2)
                        for q in range(2):
                            qs = slice(64 * q, 64 * q + 64)
                            p.mm(pbq[q][qs, e8 * 64:(e8 + 1) * 64], lhsT=H4[qs, b, jj, :],
                                 rhs=cst[qs, C_ID + 64 * q:C_ID + 64 * q + 64])
                    for q in range(2):
                        qs = slice(64 * q, 64 * q + 64)
                        p.cp(Sin[qs, g * 512:(g + 1) * 512], pbq[q][qs, :], eng="act")
                for jj_ in range(2):
                    st("sp", o_rwkv_s.rearrange("b (j q) v k -> (q v) b j k", q=2)[:, :, j0 + jj_, :], S4[:, :, jj_, :])
            else:
                pD = p.bank()
                for hh, (j, q) in enumerate(hs):
                    qs = slice(64 * q, 64 * q + 64)
                    hc = slice((2 * j + q) * 64, (2 * j + q) * 64 + 64)
                    jj = j - j0
                    p.mm(pD[qs, jj * 64:(jj + 1) * 64], lhsT=b_tok[:, hc], rhs=U_sb[:, hh, :], start=True, stop=False)
                    p.mm(pD[qs, jj * 64:(jj + 1) * 64], lhsT=k_tok[:, hc], rhs=V_tok[:, hc], start=False, stop=True)
                hv = Hst[:, j0:j0 + 2, :]
                p.tt(hv, hv, pD[:, 0:128].rearrange("p (j v) -> p j v", v=64), ALU.add)
                p.tt(hv, hv, bc(glast[:, j0:j0 + 2], [128, 2, 64], 2), ALU.mult)
                p.cp(Hbf[:, j0:j0 + 2, :], hv, eng="act")

        for pair in range(2):
            b0, b1 = 2 * pair, 2 * pair + 1
            abc(b0)
            abc(b1)
            for i in range(7):
                inv_step(b0, i)
                inv_step(b1, i)
            live = [finish(b0), finish(b1)]
            while live:
                for g_ in list(live):
                    try:
                        next(g_)
                    except StopIteration:
                        live.remove(g_)

        if last_p:
            pbq = [p.bank(), p.bank()]
            for j in range(8):
                for q in range(2):
                    qs = slice(64 * q, 64 * q + 64)
                    p.mm(pbq[q][qs, j * 64:(j + 1) * 64], lhsT=Hst[qs, j, :], rhs=cst[qs, C_ID + 64 * q:C_ID + 64 * q + 64])
            stg = BG[2][:, 0:512]
            for q in range(2):
                qs = slice(64 * q, 64 * q + 64)
                p.cp(stg[qs, :], pbq[q][qs, :], eng="act")
            st("sp", o_rwkv_p.rearrange("(j q) v k -> (q v) j k", q=2), stg.rearrange("p (j k) -> p j k", k=64))

        s1, s2 = F(1), F(2)
        y2b = HB[0][:, 0:1024]
        p.tt(y2b.rearrange("p (c t) -> p c t", t=128), ysb, ysb, ALU.mult)
        for hf in range(2):
            pm_ = p.bank()
            pq = p.bank()
            p.mm(pm_[:], lhsT=BLK, rhs=ysb[:, hf * 4:hf * 4 + 4, :].rearrange("p c t -> p (c t)"))
            p.mm(pq[:], lhsT=BLKb[:], rhs=y2b[:, hf * 512:(hf + 1) * 512])
            p.ts(s1[:, hf * 4:hf * 4 + 4, :], pm_[:].rearrange("p (c t) -> p c t", t=128), 1.0 / 64, ALU.mult)
            p.ts(s2[:, hf * 4:hf * 4 + 4, :], pq[:].rearrange("p (c t) -> p c t", t=128), 1.0 / 64, ALU.mult)
        f3 = F(3)
        p.tt(f3, s1, s1, ALU.mult)
        p.tt(s2, s2, f3, ALU.subtract)
        p.act(s2, s2, AF.Ln, bias=64e-5)
        p.act(s2, s2, AF.Exp, scale=-0.5)
        p.tt(ysb, ysb, s1, ALU.subtract)
        p.tt(ysb, ysb, s2, ALU.mult)
        p.tt(ysb, ysb, bc(par("lnw", 0, 8), [128, 8, 128], 2), ALU.mult)
        p.tt(ysb, ysb, bc(par("lnb", 0, 8), [128, 8, 128], 2), ALU.add)
        p.tt(ysb, ysb, bv, ALU.add)
        p.tt(ybT[:], ysb, G16[:], ALU.mult)

        for cb_ in range(2):
            s = w_next(("b", 512 * cb_))
            pb = p.bank()
            for cc in range(4):
                for kc in range(8):
                    p.mm(pb[:, cc * 128:(cc + 1) * 128], lhsT=s[:, kc, cc * 128:(cc + 1) * 128], rhs=ybT[:, kc, :],
                         start=(kc == 0), stop=(kc == 7))
            p.cp(ubT[:, cb_ * 4:cb_ * 4 + 4, :], pb[:].rearrange("p (c t) -> p c t", t=128), eng="act")

        mf = F(4)
        for i in range(4):
            s = w_next(("in", OFF_G + 512 * i))
            pb = p.bank()
            for cc in range(4):
                for kc in range(8):
                    p.mm(pb[:, cc * 128:(cc + 1) * 128], lhsT=s[:, kc, cc * 128:(cc + 1) * 128], rhs=hT[:, kc, :],
                         start=(kc == 0), stop=(kc == 7))
            th = F(5)[:, 0:4, :]
            p.act(th, pb[:].rearrange("p (c t) -> p c t", t=128), AF.Tanh, scale=0.5)
            cs_ = slice((i % 2) * 4, (i % 2) * 4 + 4)
            if i < 2:
                p.stt(mf[:, cs_, :], th, 1.0, uaT[:, cs_, :], ALU.add, ALU.mult)
            else:
                p.stt(th, th, 1.0, ubT[:, cs_, :], ALU.add, ALU.mult)
                p.tt(mf[:, cs_, :], mf[:, cs_, :], th, ALU.add)
                p.ts(mT[:, cs_, :], mf[:, cs_, :], 0.5, ALU.mult)

        for cb_ in range(2):
            s = w_next(("out", 512 * cb_))
            pb = p.bank()
            for kc in range(8):
                p.mm(pb[:], lhsT=mT[:, kc, :], rhs=s[:, kc, :], start=(kc == 0), stop=(kc == 7))
            p.tt(xr[:, cb_ * 512:(cb_ + 1) * 512], xr[:, cb_ * 512:(cb_ + 1) * 512], pb[:], ALU.add)

        rms_to_hT(xr[:], "n2g")

        def uv_(ub, n, j):
            if is_s:
                return ub[:, 0:n, :].rearrange("p c (b t) -> p c b t", t=10)[:, :, :, j:j + 8]
            return ub[:, 0:n, j:j + 128]

        def ffn_blk(i):
            n4 = 4 if i < 5 else 2
            ub = ugblk[i % 2]
            s = w_next(("up", 512 * i))
            if is_s:
                ncol = n4 * 128
                p.dma("sp", rowsbuf[0:32, 0:ncol], s_ffn.rearrange("b t c -> (b t) c")[:, i * 512:i * 512 + ncol])
                pbh = p.bank()
                for cc in range(n4):
                    p.tr(pbh[:, cc * 32:(cc + 1) * 32], rowsbuf[0:32, cc * 128:(cc + 1) * 128], cst[0:32, C_ID:C_ID + 32])
                p.cp(uv_(ub, n4, 0)[:, :, :, 0:2], pbh[:, 0:n4 * 32].rearrange("p (c b t) -> p c b t", c=n4, b=16), eng="act")
            else:
                p.cp(ub[:, 0:n4, 0:2], carry_ffn[:, 4 * i:4 * i + n4, :])
            pb = p.bank()
            for cc in range(n4):
                for kc in range(8):
                    p.mm(pb[:, cc * 128:(cc + 1) * 128], lhsT=s[:, kc, cc * 128:(cc + 1) * 128], rhs=hT[:, kc, :],
                         start=(kc == 0), stop=(kc == 7))
            p.cp(uv_(ub, n4, 2), tv(pb[:, 0:n4 * 128].rearrange("p (c t) -> p c t", t=128)), eng="act")
            if is_s:
                cmpb = BG[2]
                cv = cmpb[:, 0:n4 * 32].rearrange("p (c b t) -> p c b t", c=n4, b=16)
                p.cp(cv, ub[:, 0:n4, :].rearrange("p c (b t) -> p c b t", t=10)[:, :, :, 8:10])
                emit_rows(lambda c: cmpb[:, c * 32:(c + 1) * 32], n4, 32,
                          o_ffn_s.rearrange("b t c -> (b t) c")[:, i * 512:i * 512 + n4 * 128], BG[3])
            else:
                p.cp(carry_ffn[:, 4 * i:4 * i + n4, :], ub[:, 0:n4, 128:130])
            s2_ = w_next(("up", D_FF + 512 * i))
            pv2 = p.bank()
            for cc in range(n4):
                for kc in range(8):
                    p.mm(pv2[:, cc * 128:(cc + 1) * 128], lhsT=s2_[:, kc, cc * 128:(cc + 1) * 128], rhs=hT[:, kc, :],
                         start=(kc == 0), stop=(kc == 7))
            odd_ = (i % 2 == 1) and not is_s
            acc_ = F(4) if odd_ else F(0)
            th_ = F(5) if odd_ else F(1)
            yield
            for cc in range(n4):
                c = 4 * i + cc
                a3 = tv(acc_[:, cc:cc + 1, :])
                p.act(a3, uv_(ub, n4, 0)[:, cc:cc + 1] if is_s else ub[:, cc:cc + 1, 0:128], AF.Identity,
                      scale=par("fw", c), bias=par("fb", c))
                for j in (1, 2):
                    src = uv_(ub, n4, j)[:, cc:cc + 1] if is_s else ub[:, cc:cc + 1, j:j + 128]
                    p.stt(a3, src, par("fw", j * 22 + c), a3, ALU.mult, ALU.add)
            yield
            p.act(th_[:, 0:n4, :], acc_[:, 0:n4, :], AF.Tanh)
            p.stt(acc_[:, 0:n4, :], th_[:, 0:n4, :], 1.0, acc_[:, 0:n4, :], ALU.add, ALU.mult)
            p.tt(actT[:, 4 * i:4 * i + n4, :], acc_[:, 0:n4, :], pv2[:, 0:n4 * 128].rearrange("p (c t) -> p c t", t=128), ALU.mult)

        for i0_ in (0, 2, 4):
            gens_ = [ffn_blk(i0_), ffn_blk(i0_ + 1)]
            if is_s:
                for g_ in gens_:
                    for _ in g_:
                        pass
            else:
                live_ = list(gens_)
                while live_:
                    for g_ in list(live_):
                        try:
                            next(g_)
                        except StopIteration:
                            live_.remove(g_)
        if last_p:
            emit_rows(lambda c: carry_ffn[:, c, :], 22, 2, o_ffn_p, BG[3])

        for cb_ in range(2):
            pb = p.bank()
            for kb, nr_ in ((0, 8), (1, 8), (2, 6)):
                s = w_next(("down", 512 * cb_))
                for kc in range(nr_):
                    p.mm(pb[:], lhsT=actT[:, kb * 8 + kc, :], rhs=s[:, kc, :], start=(kb == 0 and kc == 0),
                         stop=(kb == 2 and kc == nr_ - 1))
            p.tt(xr[:, cb_ * 512:(cb_ + 1) * 512], xr[:, cb_ * 512:(cb_ + 1) * 512], pb[:], ALU.add)
        p.ms(sm_r[:, 2:3], 0.0)
        p.act(hb, xr[:], AF.Square, accum=sm_r[:, 2:3])
        p.act(sm_r[:, 3:4], sm_r[:, 2:3], AF.Ln, bias=1e-5, scale=1.0 / 1024)
        p.act(sm_r[:, 3:4], sm_r[:, 3:4], AF.Exp, scale=-0.5)
        p.stt(xr[:], xr[:], sm_r[:, 3:4], pbc[:, PB_FG:PB_FG + 1024], ALU.mult, ALU.mult)
        st("sp", ydst, xr[:])

    for ti in range(NPT):
        do_tile(ti, False)
    if do_sample:
        do_tile(NPT, True)
    p.final_wait("sp", out_bufs)
    p.emit()
    return nc


def make_consts():
    r = np.arange(128)[:, None]
    c = np.arange(128)[None, :]
    same = (r // 8) == (c // 8)
    m = np.zeros((128, NCST), np.float32)
    m[:, C_ID:C_ID + 128] = (r == c)
    m[:, C_SUP:C_SUP + 128] = (r < c)
    m[:, C_TRIP:C_TRIP + 128] = (r <= c)
    m[:, C_UUP:C_UUP + 128] = (r > c)
    m[:, C_SUS:C_SUS + 128] = (r < c) & same
    m[:, C_TRIS:C_TRIS + 128] = (r <= c) & same
    m[:, C_UUS:C_UUS + 128] = (r > c) & same
    m[:, C_ONES:C_ONES + 128] = 1.0
    m[:, C_BLK:C_BLK + 128] = (r // 64) == (c // 64)
    m[:, C_RST:C_RST + 128] = (c % 8 != 0)
    m[:, C_SEQ:C_SEQ + 16] = (r // 8) == np.arange(16)[None, :]
    return m


def make_inputs(core, I):
    f = lambda a: np.ascontiguousarray(a, dtype=np.float32)
    rows = [I["norm1_g"][0].reshape(8, 128), I["norm2_g"][0].reshape(8, 128),
            I["ssm_conv_w"][0].reshape(4 * 24, 128), I["ssm_conv_b"][0].reshape(24, 128),
            I["rwkv_mu"][0].reshape(26, 128), I["ffn_conv_w"][0].reshape(3 * 22, 128),
            I["ffn_conv_b"][0].reshape(22, 128), I["rwkv_w0"][0].reshape(8, 128), I["rwkv_a0"][0].reshape(8, 128),
            I["rwkv_k_k"][0].reshape(8, 128), I["rwkv_k_a"][0].reshape(8, 128), I["rwkv_r_k"][0].reshape(8, 128),
            I["rwkv_ln_w"][0].reshape(8, 128), I["rwkv_ln_b"][0].reshape(8, 128), I["ssm_norm_g"][0].reshape(16, 128)]
    pbcv = np.concatenate([I["ssm_dt_bias"][0], I["ssm_a_log"][0], I["ssm_d"][0], I["final_g"]])
    sl = slice(16 * core, 16 * core + 16)
    return {
        "xp": f(I["x_prompt"][core]), "xs": f(I["x_sample"][sl].reshape(128, 1024)),
        "s_conv": f(I["state_ssm_conv"][0, sl]), "s_ssm": f(I["state_ssm"][0, sl]),
        "s_shift": f(I["state_rwkv_shift"][0, sl]), "s_rwkv": f(I["state_rwkv"][0, sl]),
        "s_ffn": f(I["state_ffn_conv"][0, sl]),
        "w_in": f(I["w_in"][0]), "w_a": f(I["w_branch_a"][0]), "w_b": f(I["w_branch_b"][0]),
        "w_out": f(I["w_out"][0]), "w_up": f(I["ffn_w_up"][0]), "w_down": f(I["ffn_w_down"][0]),
        "lora_wa": f(np.concatenate([I["rwkv_w_up"][0], I["rwkv_a_up"][0]], 0)), "lora_g": f(I["rwkv_g_up"][0]),
        "prow": f(np.concatenate(rows, 0)), "pbc_in": f(pbcv), "cst_in": make_consts(),
    }


def kernel(**I):
    I = {k: np.asarray(v) for k, v in I.items()}
    nc = build_nc()
    in_maps = [make_inputs(c, I) for c in range(NCORES)]
    res = run_bass_kernel_spmd(nc, in_maps, core_ids=list(range(NCORES)))
    R = res.results
    g = lambda k: np.stack([np.asarray(R[c][k], dtype=np.float32) for c in range(NCORES)], 0)
    cat = lambda k: np.concatenate([np.asarray(R[c][k], dtype=np.float32) for c in range(NCORES)], 0)
    return (g("y_p"), cat("y_s").reshape(128, 8, 1024),
            g("o_conv_p")[None], g("o_ssm_p")[None], g("o_shift_p").reshape(1, 8, 3328), g("o_rwkv_p")[None],
            g("o_ffn_p")[None],
            cat("o_conv_s")[None], cat("o_ssm_s")[None], cat("o_shift_s")[None], cat("o_rwkv_s")[None],
            cat("o_ffn_s")[None])
```
